# Optimizing a Trainium2 kernel written in Bass

```python
import math
import jax
import jax.numpy as jnp
from jax import lax
import numpy as np

D_MODEL = 1024
BATCH = 1
SEQ = 16384
DEPTH = 2

CHUNK = 64
QBLOCK = 128
GROUP_WIDTH = D_MODEL // 2
DK_A = 64
DV_A = 2 * DK_A
N_HEADS_A = GROUP_WIDTH // DV_A
DH_B = 64
N_HEADS_B = GROUP_WIDTH // DH_B
LEFT_CHUNKS = 8
BAND = (LEFT_CHUNKS + 1) * CHUNK
REL_CLIP = 2 * CHUNK
NUM_BUCKETS = 32
MAX_DISTANCE = 128
DV_C = 128
DQK_C = DV_C // 2
N_HEADS_C = GROUP_WIDTH // DV_C
ROPE_BASE = 10000.0
S5_CH = GROUP_WIDTH
S5_GROUP = 16
S5_GROUPS = S5_CH // S5_GROUP
S5_STATE = 64
D_FF = ((8 * D_MODEL // 3 + 255) // 256) * 256
CONV_W = 3
N_EVEN = (DEPTH + 1) // 2
N_ODD = DEPTH // 2
EVEN_IN = 3 * N_HEADS_A * DV_A + 3 * N_HEADS_B * DH_B
ODD_IN = 2 * N_HEADS_C * DQK_C + 2 * N_HEADS_C * DV_C + S5_CH
EPS = 1e-6
NEG_INF = -1e30

kernel_name = 'hybrid_diffattn_band_retention_s5_trunk'


def rmsnorm(x, g=None):
    xf = x.astype(jnp.float32)
    y = xf * lax.rsqrt(jnp.mean(xf * xf, axis=-1, keepdims=True) + EPS)
    if g is not None:
        y = y * g.astype(jnp.float32)
    return y


def modulated_rmsnorm(x, g, shift, scale):
    y = rmsnorm(x, g) * (1.0 + scale.astype(jnp.float32)[:, None, :]) + shift.astype(jnp.float32)[:, None, :]
    return y.astype(x.dtype)


def t5_bucket(rel):
    nb = NUM_BUCKETS // 2
    max_exact = nb // 2
    bucket = jnp.where(rel > 0, nb, 0)
    n = jnp.abs(rel)
    nf = jnp.maximum(n, 1).astype(jnp.float32)
    large = max_exact + (jnp.log(nf / max_exact) / math.log(MAX_DISTANCE / max_exact) * (nb - max_exact)).astype(jnp.int32)
    large = jnp.minimum(large, nb - 1)
    return bucket + jnp.where(n < max_exact, n, large)


def diff_attention(q, k, v, t5_table, lam, lam_init, subln_g):
    B_, L = q.shape[0], q.shape[1]
    nqb = L // QBLOCK
    q = q.astype(jnp.float32) * (DK_A ** -0.5)
    k = k.astype(jnp.float32)
    v = v.astype(jnp.float32)
    table = t5_table.astype(jnp.float32)
    key_pos = jnp.arange(L, dtype=jnp.int32)
    key_chunk = key_pos // CHUNK
    q_blocks = jnp.moveaxis(q.reshape(B_, nqb, QBLOCK, N_HEADS_A, 2, DK_A), 1, 0)

    def one_block(args):
        q_blk, blk = args
        q_pos = blk * QBLOCK + jnp.arange(QBLOCK, dtype=jnp.int32)
        bias = jnp.moveaxis(table[t5_bucket(key_pos[None, :] - q_pos[:, None])], -1, 0)
        visible = key_chunk[None, :] <= (q_pos // CHUNK)[:, None]
        s = jnp.einsum('bqhmd,bkhmd->bhmqk', q_blk, k) + bias[None, :, None]
        p = jax.nn.softmax(jnp.where(visible, s, NEG_INF), axis=-1)
        w = p[:, :, 0] - lam * p[:, :, 1]
        return jnp.einsum('bhqk,bkhd->bqhd', w, v)

    out = lax.map(one_block, (q_blocks, jnp.arange(nqb, dtype=jnp.int32)))
    out = jnp.moveaxis(out, 0, 1).reshape(B_, L, N_HEADS_A, DV_A)
    out = rmsnorm(out, subln_g) * (1.0 - lam_init)
    return out.reshape(B_, L, N_HEADS_A * DV_A)


def band_attention(q, k, v, rel_bias):
    B_, L = q.shape[0], q.shape[1]
    nc = L // CHUNK
    shp = (B_, nc, CHUNK, N_HEADS_B, DH_B)
    qc = q.astype(jnp.float32).reshape(shp) * (DH_B ** -0.5)

    def gather_band(t):
        tp = jnp.pad(t.astype(jnp.float32).reshape(shp), ((0, 0), (LEFT_CHUNKS, 0), (0, 0), (0, 0), (0, 0)))
        return jnp.concatenate([tp[:, j:j + nc] for j in range(LEFT_CHUNKS + 1)], axis=2)

    kb = gather_band(k)
    vb = gather_band(v)
    m = jnp.arange(BAND, dtype=jnp.int32)
    i = jnp.arange(CHUNK, dtype=jnp.int32)
    rel = jnp.clip(m[None, :] - LEFT_CHUNKS * CHUNK - i[:, None], -REL_CLIP, REL_CLIP) + REL_CLIP
    bias = rel_bias.astype(jnp.float32)[:, rel]
    valid = (jnp.arange(nc, dtype=jnp.int32)[:, None] - LEFT_CHUNKS + m[None, :] // CHUNK) >= 0
    s = jnp.einsum('bnqhd,bnkhd->bhnqk', qc, kb) + bias[None, :, None]
    p = jax.nn.softmax(jnp.where(valid[None, None, :, None, :], s, NEG_INF), axis=-1)
    out = jnp.einsum('bhnqk,bnkhd->bnqhd', p, vb)
    return out.reshape(B_, L, N_HEADS_B * DH_B)


def rotary(t):
    L, d = t.shape[1], t.shape[-1]
    inv_freq = 1.0 / (ROPE_BASE ** (jnp.arange(0, d, 2, dtype=jnp.float32) / d))
    ang = jnp.arange(L, dtype=jnp.float32)[:, None] * inv_freq[None, :]
    cos = jnp.cos(ang)[None, :, None, :]
    sin = jnp.sin(ang)[None, :, None, :]
    t1, t2 = jnp.split(t, 2, axis=-1)
    return jnp.concatenate([t1 * cos - t2 * sin, t1 * sin + t2 * cos], axis=-1)


def retention(q, k, v):
    B_, L = q.shape[0], q.shape[1]
    nc = L // CHUNK
    log_g = jnp.log(1.0 - jnp.power(2.0, -5.0 - jnp.arange(N_HEADS_C, dtype=jnp.float32)))
    pos = jnp.arange(CHUNK, dtype=jnp.float32)
    intra_decay = jnp.exp(log_g[:, None, None] * jnp.abs(pos[:, None] - pos[None, :]))
    q_decay = jnp.exp(log_g[:, None] * (pos[None, :] + 1.0))
    k_decay = jnp.exp(log_g[:, None] * (CHUNK - 1.0 - pos[None, :]))
    chunk_decay = jnp.exp(log_g * CHUNK)
    qc = q.reshape(B_, nc, CHUNK, N_HEADS_C, DQK_C)
    kc = k.reshape(B_, nc, CHUNK, N_HEADS_C, DQK_C) * (DQK_C ** -0.5)
    vc = v.reshape(B_, nc, CHUNK, N_HEADS_C, DV_C)
    scores = jnp.einsum('bnihd,bnjhd->bnhij', qc, kc) * intra_decay
    intra = jnp.einsum('bnhij,bnjhe->bnihe', scores, vc)
    kv = jnp.einsum('bnjhd,hj,bnjhe->bnhde', kc, k_decay, vc)

    def step(state, kv_n):
        return state * chunk_decay[None, :, None, None] + kv_n, state

    init = jnp.zeros((B_, N_HEADS_C, DQK_C, DV_C), jnp.float32)
    _, s_prev = lax.scan(step, init, jnp.moveaxis(kv, 1, 0))
    s_prev = jnp.moveaxis(s_prev, 0, 1)
    cross = jnp.einsum('bnihd,hi,bnhde->bnihe', qc, q_decay, s_prev)
    return (intra + cross).reshape(B_, L, N_HEADS_C, DV_C)


def _ssm_combine(e1, e2):
    a1, b1 = e1
    a2, b2 = e2
    return a1 * a2, a2 * b1 + b2


def s5_ssm(u, lam_re, lam_im, log_step, b_re, b_im, c_re, c_im, d_skip):
    B_, L = u.shape[0], u.shape[1]
    uf = u.astype(jnp.float32).reshape(B_, L, S5_GROUPS, S5_GROUP)
    lam = lax.complex(lam_re.astype(jnp.float32), lam_im.astype(jnp.float32))
    step = jnp.exp(log_step.astype(jnp.float32))[:, None]
    a_bar = jnp.exp(lam * step)
    b = lax.complex(b_re.astype(jnp.float32), b_im.astype(jnp.float32))
    b_bar = ((a_bar - 1.0) / lam)[..., None] * b
    bu = jnp.einsum('gnp,blgp->blgn', b_bar, uf.astype(jnp.complex64))
    a_seq = jnp.broadcast_to(a_bar, bu.shape)
    _, states = lax.associative_scan(_ssm_combine, (a_seq, bu), axis=1)
    cm = lax.complex(c_re.astype(jnp.float32), c_im.astype(jnp.float32))
    y = jnp.einsum('gpn,blgn->blgp', cm, states).real + d_skip.astype(jnp.float32).reshape(S5_GROUPS, S5_GROUP) * uf
    return y.reshape(B_, L, S5_CH)


def even_mixer(h, w_in, w_out, t5_table, lam_params, subln_g, rel_bias, lam_init):
    B_, L = h.shape[0], h.shape[1]
    wa = N_HEADS_A * DV_A
    wb = N_HEADS_B * DH_B
    proj = h @ w_in
    qa, ka, va, qb, kb, vb = jnp.split(proj, [wa, 2 * wa, 3 * wa, 3 * wa + wb, 3 * wa + 2 * wb], axis=-1)
    lp = lam_params.astype(jnp.float32)
    lam = jnp.exp(jnp.sum(lp[0] * lp[1])) - jnp.exp(jnp.sum(lp[2] * lp[3])) + lam_init
    out_a = diff_attention(qa.reshape(B_, L, N_HEADS_A, 2, DK_A), ka.reshape(B_, L, N_HEADS_A, 2, DK_A),
                           va.reshape(B_, L, N_HEADS_A, DV_A), t5_table, lam, lam_init, subln_g)
    out_b = band_attention(qb.reshape(B_, L, N_HEADS_B, DH_B), kb.reshape(B_, L, N_HEADS_B, DH_B),
                           vb.reshape(B_, L, N_HEADS_B, DH_B), rel_bias)
    return jnp.concatenate([out_a, out_b], axis=-1).astype(h.dtype) @ w_out


def odd_mixer(h, w_in, w_out, lam_re, lam_im, log_step, b_re, b_im, c_re, c_im, d_skip, glu_w):
    B_, L = h.shape[0], h.shape[1]
    wqk = N_HEADS_C * DQK_C
    wv = N_HEADS_C * DV_C
    proj = h @ w_in
    q, k, v, gate, u = jnp.split(proj, [wqk, 2 * wqk, 2 * wqk + wv, 2 * wqk + 2 * wv], axis=-1)
    q = rotary(q.astype(jnp.float32).reshape(B_, L, N_HEADS_C, DQK_C))
    k = rotary(k.astype(jnp.float32).reshape(B_, L, N_HEADS_C, DQK_C))
    r = retention(q, k, v.astype(jnp.float32).reshape(B_, L, N_HEADS_C, DV_C))
    r = rmsnorm(r).reshape(B_, L, wv) * jax.nn.silu(gate.astype(jnp.float32))
    y = jax.nn.gelu(s5_ssm(u, lam_re, lam_im, log_step, b_re, b_im, c_re, c_im, d_skip))
    ga, gb = jnp.split(y.astype(h.dtype) @ glu_w, 2, axis=-1)
    y = ga.astype(jnp.float32) * jax.nn.sigmoid(gb.astype(jnp.float32))
    return jnp.concatenate([r, y], axis=-1).astype(h.dtype) @ w_out


def conv_ffn(h, w_in, conv_w, conv_b, w_out):
    up = h @ w_in
    val, gate = jnp.split(up, 2, axis=-1)
    gate = lax.conv_general_dilated(gate, conv_w[:, None, :], (1,), [(CONV_W - 1, 0)],
                                    dimension_numbers=('NWC', 'WIO', 'NWC'),
                                    feature_group_count=D_FF) + conv_b
    return (jax.nn.gelu(gate) * val) @ w_out


def setup_inputs(seed: int = 0) -> dict:
    key = jax.random.key(seed)
    ks = jax.random.split(key, 32)
    f32 = jnp.float32

    def nrm(i, shape, scale):
        return scale * jax.random.normal(ks[i], shape, f32)

    n_idx = jnp.arange(S5_STATE, dtype=f32)
    return {
        'x': nrm(0, (BATCH, SEQ, D_MODEL), 1.0),
        'c': nrm(1, (BATCH, D_MODEL), 1.0),
        't5_table': nrm(2, (NUM_BUCKETS, N_HEADS_A), 0.5),
        'mod_w': nrm(3, (DEPTH, D_MODEL, 6 * D_MODEL), 0.5 * D_MODEL ** -0.5),
        'mod_b': nrm(4, (DEPTH, 6 * D_MODEL), 0.02),
        'norm1_g': 1.0 + nrm(5, (DEPTH, D_MODEL), 0.02),
        'norm2_g': 1.0 + nrm(6, (DEPTH, D_MODEL), 0.02),
        'ffn_w_in': nrm(7, (DEPTH, D_MODEL, 2 * D_FF), D_MODEL ** -0.5),
        'ffn_conv_w': nrm(8, (DEPTH, CONV_W, D_FF), CONV_W ** -0.5),
        'ffn_conv_b': nrm(9, (DEPTH, D_FF), 0.02),
        'ffn_w_out': nrm(10, (DEPTH, D_FF, D_MODEL), D_FF ** -0.5),
        'ev_w_in': nrm(11, (N_EVEN, D_MODEL, EVEN_IN), D_MODEL ** -0.5),
        'ev_w_out': nrm(12, (N_EVEN, 2 * GROUP_WIDTH, D_MODEL), (2 * GROUP_WIDTH) ** -0.5),
        'diff_lambda': nrm(13, (N_EVEN, 4, DK_A), 0.1),
        'diff_subln_g': 1.0 + nrm(14, (N_EVEN, DV_A), 0.02),
        'band_rel_bias': nrm(15, (N_EVEN, N_HEADS_B, 2 * REL_CLIP + 1), 0.5),
        'od_w_in': nrm(16, (N_ODD, D_MODEL, ODD_IN), D_MODEL ** -0.5),
        'od_w_out': nrm(17, (N_ODD, 2 * GROUP_WIDTH, D_MODEL), (2 * GROUP_WIDTH) ** -0.5),
        's5_lam_re': -0.5 + nrm(18, (N_ODD, S5_GROUPS, S5_STATE), 0.01),
        's5_lam_im': math.pi * n_idx + nrm(19, (N_ODD, S5_GROUPS, S5_STATE), 0.01),
        's5_log_step': jax.random.uniform(ks[20], (N_ODD, S5_GROUPS), f32, math.log(1e-3), math.log(1e-1)),
        's5_b_re': nrm(21, (N_ODD, S5_GROUPS, S5_STATE, S5_GROUP), (2 * S5_GROUP) ** -0.5),
        's5_b_im': nrm(22, (N_ODD, S5_GROUPS, S5_STATE, S5_GROUP), (2 * S5_GROUP) ** -0.5),
        's5_c_re': nrm(23, (N_ODD, S5_GROUPS, S5_GROUP, S5_STATE), (2 * S5_STATE) ** -0.5),
        's5_c_im': nrm(24, (N_ODD, S5_GROUPS, S5_GROUP, S5_STATE), (2 * S5_STATE) ** -0.5),
        's5_d': nrm(25, (N_ODD, S5_CH), 1.0),
        's5_glu_w': nrm(26, (N_ODD, S5_CH, 2 * S5_CH), S5_CH ** -0.5),
        'final_g': 1.0 + nrm(27, (D_MODEL,), 0.02),
    }


def reference(x, c, t5_table, mod_w, mod_b, norm1_g, norm2_g, ffn_w_in, ffn_conv_w, ffn_conv_b, ffn_w_out,
              ev_w_in, ev_w_out, diff_lambda, diff_subln_g, band_rel_bias,
              od_w_in, od_w_out, s5_lam_re, s5_lam_im, s5_log_step, s5_b_re, s5_b_im, s5_c_re, s5_c_im,
              s5_d, s5_glu_w, final_g):
    cond = jax.nn.silu(c)
    for i in range(DEPTH):
        mod = cond @ mod_w[i] + mod_b[i]
        sh1, sc1, g1, sh2, sc2, g2 = jnp.split(mod, 6, axis=-1)
        h = modulated_rmsnorm(x, norm1_g[i], sh1, sc1)
        if i % 2 == 0:
            e = i // 2
            lam_init = 0.8 - 0.6 * math.exp(-0.3 * i)
            mixed = even_mixer(h, ev_w_in[e], ev_w_out[e], t5_table, diff_lambda[e], diff_subln_g[e],
                               band_rel_bias[e], lam_init)
        else:
            o = i // 2
            mixed = odd_mixer(h, od_w_in[o], od_w_out[o], s5_lam_re[o], s5_lam_im[o], s5_log_step[o],
                              s5_b_re[o], s5_b_im[o], s5_c_re[o], s5_c_im[o], s5_d[o], s5_glu_w[o])
        x = x + g1[:, None, :] * mixed
        h = modulated_rmsnorm(x, norm2_g[i], sh2, sc2)
        x = x + g2[:, None, :] * conv_ffn(h, ffn_w_in[i], ffn_conv_w[i], ffn_conv_b[i], ffn_w_out[i])
    return rmsnorm(x, final_g).astype(x.dtype)
```

```python
import contextlib
import numpy as np
import concourse.bass as bass
import concourse.mybir as mybir
from concourse.bass_utils import run_bass_kernel_spmd

F32 = mybir.dt.float32
BF16 = mybir.dt.bfloat16
I32 = mybir.dt.int32
ALU = mybir.AluOpType
AF = mybir.ActivationFunctionType
AX = mybir.AxisListType

ENGS = ("pe", "act", "dve", "pool", "sp")


class Reg:
    __slots__ = ("name", "w", "r", "sem", "cnt")

    def __init__(self, name=""):
        self.name = name
        self.w = None
        self.r = []
        self.sem = None
        self.cnt = 0


class Prog:
    def __init__(self):
        self.nc = bass.Bass("TRN2", target_bir_lowering=False)
        self.stack = contextlib.ExitStack()
        self.esem = {}
        for e in ENGS:
            self.esem[e] = self.stack.enter_context(self.nc.semaphore("es_" + e))
        self.q = {e: [] for e in ENGS}
        self.done = {e: 0 for e in ENGS}
        self.waited = {e: {} for e in ENGS}
        self.dma_events = []
        self.uid = 0
        self.phase_stack = None
        self.nsem = 0

    def dram(self, name, shape, dt, kind="Internal"):
        return self.nc.dram_tensor(name, list(shape), dt, kind=kind)

    def sb(self, shape, dt, name=None):
        self.uid += 1
        t = self.phase_stack.enter_context(
            self.nc.sbuf_tensor(name or f"sb{self.uid}", list(shape), dt))
        return t

    def ps(self, shape, dt=F32, name=None):
        self.uid += 1
        t = self.phase_stack.enter_context(
            self.nc.psum_tensor(name or f"ps{self.uid}", list(shape), dt))
        return t

    def gsb(self, shape, dt, name=None):
        self.uid += 1
        return self.stack.enter_context(
            self.nc.sbuf_tensor(name or f"gsb{self.uid}", list(shape), dt))

    def _deps(self, eng, reads, writes, nowaw=False):
        deps = []
        for r in reads:
            if r.w is not None:
                deps.append(r.w)
        for w in writes:
            if w.w is not None and not (nowaw and w.w[0] == "d"):
                deps.append(w.w)
            deps.extend(w.r)
        out = []
        for d in deps:
            if d[0] == "c" and d[1] == eng:
                if eng in ("pe", "sp"):
                    continue
            out.append(d)
        return out

    def op(self, eng, fn, reads=(), writes=()):
        reads = [r for r in reads if r is not None]
        writes = [w for w in writes if w is not None]
        deps = self._deps(eng, reads, writes)
        idx = len(self.q[eng])
        rec = dict(fn=fn, deps=deps, need=False, kind="c")
        self.q[eng].append(rec)
        ev = ("c", eng, rec)
        for r in reads:
            r.r.append(ev)
        for w in writes:
            w.w = ev
            w.r = []
        return ev

    def dma(self, eng, out_ap, in_ap, reads=(), writes=(), nowaw=True, **kw):
        reads = [r for r in reads if r is not None]
        writes = [w for w in writes if w is not None]
        assert len(writes) == 1
        W = writes[0]
        deps = self._deps(eng, reads, writes, nowaw=nowaw)
        if eng == "pool":
            hist = self.__dict__.setdefault("pool_hist", [])
            if False:
                deps.append(hist[-2])
        if W.sem is None:
            self.nsem += 1
            W.sem = self.stack.enter_context(self.nc.semaphore(f"ds{self.nsem}"))
        W.cnt += 1
        val = 16 * W.cnt
        sem = W.sem

        def fn(e, out_ap=out_ap, in_ap=in_ap, sem=sem, kw=kw):
            return e.dma_start(out=out_ap, in_=in_ap, **kw).then_inc(sem, 16)
        rec = dict(fn=fn, deps=deps, need=False, kind="d")
        self.q[eng].append(rec)
        ev = ("d", sem, val)
        for r in reads:
            r.r.append(ev)
        W.w = ev
        W.r = []
        self.dma_events.append(ev)
        if eng == "pool":
            self.pool_hist.append(ev)
        return ev

    def barrier(self):
        evs = list(self.dma_events)
        for e in ENGS:
            if self.q[e]:
                last = None
                for rec in reversed(self.q[e]):
                    if rec["kind"] == "c":
                        last = rec
                        break
                if last is not None:
                    evs.append(("c", e, last))
        self.dma_events = []
        for e in ENGS:
            deps = [d for d in evs if not (d[0] == "c" and d[1] == e)]
            rec = dict(fn=None, deps=deps, need=False, kind="n")
            self.q[e].append(rec)

    def begin_phase(self):
        assert self.phase_stack is None
        self.phase_stack = contextlib.ExitStack()

    def end_phase(self, barrier=True):
        if barrier:
            self.barrier()
        for e in ENGS:
            for rec in self.q[e]:
                for d in rec["deps"]:
                    if d[0] == "c":
                        d[2]["need"] = True
        for e in ENGS:
            v = self.done[e]
            for rec in self.q[e]:
                if rec["kind"] == "c" and rec["need"]:
                    v += 1
                    rec["val"] = v
            self.done[e] = v
            for rec in self.q[e]:
                if rec["kind"] == "c" and not rec["need"]:
                    rec["val"] = v
        nc = self.nc
        prog = self
        with nc.Block() as block:
            def emit(ename, eh):
                waited = prog.waited[ename]
                for rec in prog.q[ename]:
                    want = {}
                    for d in rec["deps"]:
                        if d[0] == "c":
                            sem = prog.esem[d[1]]
                            val = d[2]["val"]
                        else:
                            sem, val = d[1], d[2]
                        k = id(sem)
                        if want.get(k, (None, 0))[1] < val:
                            want[k] = (sem, val)
                    for k, (sem, val) in want.items():
                        if waited.get(k, 0) < val:
                            eh.wait_ge(sem, val)
                            waited[k] = val
                    if rec["fn"] is None:
                        continue
                    ins = rec["fn"](eh)
                    if rec["kind"] == "c" and rec["need"]:
                        ins.then_inc(prog.esem[ename], 1)

            @block.tensor
            def _(e):
                emit("pe", e)

            @block.scalar
            def _(e):
                emit("act", e)

            @block.vector
            def _(e):
                emit("dve", e)

            @block.gpsimd
            def _(e):
                emit("pool", e)

            @block.sync
            def _(e):
                emit("sp", e)
        self.q = {e: [] for e in ENGS}
        self.phase_stack.close()
        self.phase_stack = None

    def finish(self):
        self.stack.close()
        return self.nc


import math
import numpy as np

NCORE = 8
L = 16384
TL = 2048
D = 1024
EPS = 1e-6
NEG = -1.0e30


def consts(P):
    c = {}
    c["ones32"] = P.sb([128, 128], F32); c["r_ones32"] = Reg()
    c["ones16"] = P.sb([128, 128], BF16); c["r_ones16"] = Reg()
    c["eps"] = P.sb([128, 1], F32); c["r_eps"] = Reg()
    P.op("pool", lambda e: e.memset(c["ones32"][:], 1.0), writes=[c["r_ones32"]])
    P.op("pool", lambda e: e.memset(c["ones16"][:], 1.0), writes=[c["r_ones16"]])
    P.op("pool", lambda e: e.memset(c["eps"][:], EPS), writes=[c["r_eps"]])
    return c


def load_cast(P, dst, rdst, src_ap, K, N, col0=0, ncol=None):
    ncol = ncol or N
    for kc in range(K // 128):
        P.dma("pool", dst[:, kc, 0:ncol], src_ap[kc * 128:(kc + 1) * 128, col0:col0 + ncol],
              writes=[rdst], max_dma_last_dim=8192)


def modnorm_ws(P):
    ws = {}
    ws["sq"] = [P.sb([128, 512], F32) for _ in range(2)]; ws["rsq"] = [Reg(), Reg()]
    ws["rs"] = P.sb([128, 512], F32); ws["rrs"] = Reg()
    ws["tmp"] = [P.sb([128, 512], F32) for _ in range(2)]; ws["rtmp"] = [Reg(), Reg()]
    ws["ps"] = P.ps([128, 512]); ws["rps"] = Reg()
    return ws


def modnorm(P, C, xT, rx, ntok, gm, sh, rmod, hT, rh, tok0=0, nkc=8, ws=None, out0=0):
    t = 0
    if ws is None:
        ws = modnorm_ws(P)
    sq, rsq, rs, rrs, tmp, rtmp, ps, rps = ws["sq"], ws["rsq"], ws["rs"], ws["rrs"], ws["tmp"], ws["rtmp"], ws["ps"], ws["rps"]
    while t < ntok:
        n = min(512, ntok - t)
        a, b = tok0 + t, tok0 + t + n
        for kc in range(nkc):
            s_, r_ = sq[kc % 2], rsq[kc % 2]
            P.op("act", lambda e, a=a, b=b, n=n, kc=kc, s_=s_: e.activation(out=s_[:, 0:n], in_=xT[:, kc, a:b], func=AF.Square),
                 reads=[rx], writes=[r_])
            P.op("pe", lambda e, kc=kc, n=n, s_=s_: e.matmul(ps[:, 0:n], C["ones32"][:], s_[:, 0:n], start=(kc == 0), stop=(kc == nkc - 1)),
                 reads=[r_, C["r_ones32"]], writes=[rps])
        P.op("act", lambda e, n=n: e.activation(out=rs[:, 0:n], in_=ps[:, 0:n], func=AF.Sqrt, scale=1.0 / (128 * nkc), bias=C["eps"][:]),
             reads=[rps, C["r_eps"]], writes=[rrs])
        P.op("dve", lambda e, n=n: e.reciprocal(rs[:, 0:n], rs[:, 0:n]), reads=[rrs], writes=[rrs])
        for kc in range(nkc):
            t_, r_ = tmp[kc % 2], rtmp[kc % 2]
            P.op("dve", lambda e, kc=kc, a=a, b=b, n=n, t_=t_: e.scalar_tensor_tensor(
                out=t_[:, 0:n], in0=xT[:, kc, a:b], scalar=gm[:, kc:kc + 1], in1=rs[:, 0:n], op0=ALU.mult, op1=ALU.mult),
                reads=[rx, rrs, rmod], writes=[r_])
            P.op("act", lambda e, kc=kc, t=t, n=n, t_=t_: e.activation(out=hT[:, kc, out0 + t:out0 + t + n], in_=t_[:, 0:n], func=AF.Identity,
                                                               bias=sh[:, kc:kc + 1], scale=1.0),
                 reads=[r_, rmod], writes=[rh])
        t += n


def mod_prep(P, modT_d, g_d, which):
    o = 0 if which == 1 else 24
    mt = P.sb([128, 48], F32); r = Reg()
    gt = P.sb([128, 8], F32)
    gm = P.sb([128, 8], F32)
    P.dma("sp", mt[:], modT_d.ap(), writes=[r])
    P.dma("sp", gt[:], g_d.ap(), writes=[r])
    P.op("dve", lambda e: e.scalar_tensor_tensor(out=gm[:], in0=mt[:, o + 8:o + 16], scalar=1.0, in1=gt[:], op0=ALU.add, op1=ALU.mult),
         reads=[r], writes=[r])
    return gm, mt[:, o:o + 8], mt[:, o + 16:o + 24], r


def build_s0():
    P = Prog()
    cT = P.dram("cT", [128, 8], F32, kind="ExternalInput")
    mw = P.dram("mw", [2, 1024, 768], F32, kind="ExternalInput")
    mb = P.dram("mb", [2, 768], F32, kind="ExternalInput")
    out = P.dram("out", [2, 768], F32, kind="ExternalOutput")
    P.begin_phase()
    ct = P.sb([128, 8], F32); rc = Reg()
    P.dma("sp", ct[:], cT.ap(), writes=[rc])
    P.op("act", lambda e: e.activation(out=ct[:], in_=ct[:], func=AF.Silu), reads=[rc], writes=[rc])
    bt = P.sb([1, 2, 768], F32); rb = Reg()
    P.dma("sp", bt[:], mb.ap().rearrange("(o i) n -> o i n", o=1), writes=[rb])
    wt = [P.sb([128, 8, 384], F32) for _ in range(2)]; rw = [Reg(), Reg()]
    ps = [P.ps([1, 384]) for _ in range(2)]; rp = [Reg(), Reg()]
    ot = P.sb([1, 2, 768], F32); ro = Reg(); rout = Reg()
    k = 0
    for i in range(2):
        for cb in range(2):
            w = wt[k % 2]; r = rw[k % 2]; p = ps[k % 2]; rpp = rp[k % 2]
            for kc in range(8):
                P.dma("sp", w[:, kc, :], mw.ap()[i, kc * 128:(kc + 1) * 128, cb * 384:(cb + 1) * 384], writes=[r])
            for kc in range(8):
                P.op("pe", lambda e, w=w, p=p, kc=kc: e.matmul(p[:], ct[:, kc:kc + 1], w[:, kc, :], start=(kc == 0), stop=(kc == 7)),
                     reads=[r, rc], writes=[rpp])
            P.op("dve", lambda e, p=p, i=i, cb=cb: e.tensor_tensor(out=ot[:, i, cb * 384:(cb + 1) * 384], in0=p[:], in1=bt[:, i, cb * 384:(cb + 1) * 384], op=ALU.add),
                 reads=[rpp, rb], writes=[ro])
            k += 1
    P.dma("sp", out.ap().rearrange("(o i) n -> o i n", o=1), ot[:], reads=[ro], writes=[rout])
    P.op("sp", lambda e: e.nop(), reads=[rout])
    P.end_phase()
    return P.finish()


def build_s1():
    P = Prog()
    xT_d = P.dram("xT", [1024, TL], F32, kind="ExternalInput")
    modT_d = P.dram("modT", [128, 48], F32, kind="ExternalInput")
    g_d = P.dram("g", [128, 8], F32, kind="ExternalInput")
    w_d = P.dram("w", [1024, 3072], F32, kind="ExternalInput")
    kT_o = P.dram("kT", [8, 128, TL], BF16, kind="ExternalOutput")
    v_o = P.dram("v", [16, 128, 1024], BF16, kind="ExternalOutput")
    P.begin_phase()
    C = consts(P)
    xT = P.sb([128, 8, TL], F32); rx = Reg()
    for kc in range(8):
        P.dma("sp", xT[:, kc, :], xT_d.ap()[kc * 128:(kc + 1) * 128, :], writes=[rx])
    gm, sh, gate, rmod = mod_prep(P, modT_d, g_d, 1)
    hT = P.sb([128, 8, TL], BF16); rh = Reg()
    modnorm(P, C, xT, rx, TL, gm, sh, rmod, hT, rh)
    wk = P.sb([128, 8, 1024], BF16); rwk = Reg()
    wv = P.sb([128, 8, 1024], BF16); rwv = Reg()
    load_cast(P, wk[:, :, 0:512], rwk, w_d.ap(), 1024, 3072, 512, 512)
    load_cast(P, wk[:, :, 512:1024], rwk, w_d.ap(), 1024, 3072, 2048, 512)
    load_cast(P, wv[:, :, 0:512], rwv, w_d.ap(), 1024, 3072, 1024, 512)
    load_cast(P, wv[:, :, 512:1024], rwv, w_d.ap(), 1024, 3072, 2560, 512)
    ps = [P.ps([128, 512]) for _ in range(4)]; rp = [Reg() for _ in range(4)]
    ko = P.sb([128, 2, TL], BF16); rkos = [Reg(), Reg()]
    rk_out = Reg()
    k = 0
    for mc in range(8):
        rko = rkos[mc % 2]
        for tt in range(4):
            p = ps[k % 4]; r = rp[k % 4]; k += 1
            for kc in range(8):
                P.op("pe", lambda e, p=p, mc=mc, tt=tt, kc=kc: e.matmul(p[:], wk[:, kc, mc * 128:(mc + 1) * 128], hT[:, kc, tt * 512:(tt + 1) * 512], start=(kc == 0), stop=(kc == 7)),
                     reads=[rwk, rh], writes=[r])
            eng = "act" if k % 2 else "dve"
            if eng == "act":
                P.op("act", lambda e, p=p, mc=mc, tt=tt: e.activation(out=ko[:, mc % 2, tt * 512:(tt + 1) * 512], in_=p[:], func=AF.Copy), reads=[r], writes=[rko])
            else:
                P.op("dve", lambda e, p=p, mc=mc, tt=tt: e.tensor_copy(out=ko[:, mc % 2, tt * 512:(tt + 1) * 512], in_=p[:]), reads=[r], writes=[rko])
        P.dma("sp", kT_o.ap()[mc], ko[:, mc % 2, :], reads=[rko], writes=[rk_out])
    vo = P.sb([128, 2, 1024], BF16); rvos = [Reg(), Reg()]
    rv_out = Reg()
    for tb in range(16):
        rvo = rvos[tb % 2]
        for cb in range(2):
            p = ps[k % 4]; r = rp[k % 4]; k += 1
            for kc in range(8):
                P.op("pe", lambda e, p=p, tb=tb, cb=cb, kc=kc: e.matmul(p[:], hT[:, kc, tb * 128:(tb + 1) * 128], wv[:, kc, cb * 512:(cb + 1) * 512], start=(kc == 0), stop=(kc == 7)),
                     reads=[rwv, rh], writes=[r])
            if k % 2:
                P.op("act", lambda e, p=p, tb=tb, cb=cb: e.activation(out=vo[:, tb % 2, cb * 512:(cb + 1) * 512], in_=p[:], func=AF.Copy), reads=[r], writes=[rvo])
            else:
                P.op("dve", lambda e, p=p, tb=tb, cb=cb: e.tensor_copy(out=vo[:, tb % 2, cb * 512:(cb + 1) * 512], in_=p[:]), reads=[r], writes=[rvo])
        P.dma("sp", v_o.ap()[tb], vo[:, tb % 2, :], reads=[rvo], writes=[rv_out])
    P.op("sp", lambda e: e.nop(), reads=[rk_out, rv_out])
    P.end_phase()
    return P.finish()


def build_s2(skip=()):
    P = Prog()
    din = lambda n, s, dt=F32: P.dram(n, s, dt, kind="ExternalInput")
    xT_d = din("xT", [1024, TL]); modT_d = din("modT", [128, 48]); g_d = din("g", [128, 8])
    w_d = din("w", [1024, 3072]); wo_d = din("wo", [1024, 1024])
    kall_d = din("kall", [4, 128, L], BF16); vall_d = din("vall", [4, 128, 128 * 128], BF16)
    kloc_d = din("kloc", [4, 128, 2560], BF16); vloc_d = din("vloc", [4, 128, 20 * 128], BF16)
    kbloc_d = din("kbloc", [4, 128, 2560], BF16); vbloc_d = din("vbloc", [128, 20 * 512], BF16)
    MA_d = din("MA", [4, 128, 1024]); maskA_d = din("maskA", [128, 1024])
    colA_d = din("colA", [128, 128]); c15_d = din("c15", [128, 4]); colL_d = din("colL", [128, 20])
    MB_d = din("MB", [8, 128, 1408]); maskB_d = din("maskB", [128, 1408])
    lamT_d = din("lamT", [64, 4]); sg_d = din("sg", [128, 1])
    xo_d = P.dram("xo", [1024, TL], F32, kind="ExternalOutput")

    qaT = P.gsb([128, 4, TL], BF16); rqa = Reg()
    qbT = P.gsb([128, 4, TL], BF16); rqb = Reg()
    catT = P.gsb([128, 8, TL], BF16); rcat = Reg()

    P.begin_phase()
    C = consts(P)
    xT = P.sb([128, 8, TL], F32); rx = Reg()
    for kc in range(8):
        P.dma("sp", xT[:, kc, :], xT_d.ap()[kc * 128:(kc + 1) * 128, :], writes=[rx])
    gm, sh, gate, rmod = mod_prep(P, modT_d, g_d, 1)
    hT = P.sb([128, 8, TL], BF16); rh = Reg()
    modnorm(P, C, xT, rx, TL, gm, sh, rmod, hT, rh)
    wq = P.sb([128, 8, 1024], BF16); rwq = Reg()
    load_cast(P, wq[:, :, 0:512], rwq, w_d.ap(), 1024, 3072, 0, 512)
    load_cast(P, wq[:, :, 512:1024], rwq, w_d.ap(), 1024, 3072, 1536, 512)
    ps = [P.ps([128, 512]) for _ in range(4)]; rp = [Reg() for _ in range(4)]
    k = 0
    for mc in range(8):
        dst, rd = (qaT, rqa) if mc < 4 else (qbT, rqb)
        for tt in range(4):
            p = ps[k % 4]; r = rp[k % 4]; k += 1
            for kc in range(8):
                P.op("pe", lambda e, p=p, mc=mc, tt=tt, kc=kc: e.matmul(p[:], wq[:, kc, mc * 128:(mc + 1) * 128], hT[:, kc, tt * 512:(tt + 1) * 512], start=(kc == 0), stop=(kc == 7)),
                     reads=[rwq, rh], writes=[r])
            if k % 2:
                P.op("act", lambda e, p=p, dst=dst, mc=mc, tt=tt: e.activation(out=dst[:, mc % 4, tt * 512:(tt + 1) * 512], in_=p[:], func=AF.Copy), reads=[r], writes=[rd])
            else:
                P.op("dve", lambda e, p=p, dst=dst, mc=mc, tt=tt: e.tensor_copy(out=dst[:, mc % 4, tt * 512:(tt + 1) * 512], in_=p[:]), reads=[r], writes=[rd])
    P.end_phase()

    if 'b' not in skip:
      P.begin_phase()
      C = consts(P)
      kb = P.sb([128, 4, 2560], BF16); rkb = Reg()
      vb = P.sb([128, 20, 512], BF16); rvb = Reg()
      for c in range(4):
          P.dma("sp", kb[:, c, :], kbloc_d.ap()[c], writes=[rkb])
      P.dma("sp", vb[:], vbloc_d.ap().rearrange("p (b n) -> p b n", n=512), writes=[rvb])
      colL = P.sb([128, 20], F32); rcl = Reg()
      P.dma("sp", colL[:], colL_d.ap(), writes=[rcl])
      maskB = P.sb([128, 1408], F32); rmb = Reg()
      P.dma("sp", maskB[:], maskB_d.ap(), writes=[rmb])
      onesp = P.sb([128, 2, 128], BF16); rop = Reg()
      P.op("pool", lambda e: e.memset(onesp[:], 0.0), writes=[rop])
      P.op("pool", lambda e: e.memset(onesp[:, 0, 0:64], 1.0), writes=[rop])
      P.op("pool", lambda e: e.memset(onesp[:, 1, 64:128], 1.0), writes=[rop])
      vpad = P.sb([128, 2, 20, 128], BF16); rvp = Reg()
      P.op("pool", lambda e: e.memset(vpad[:], 0.0), writes=[rvp])
      Mf = P.sb([128, 2, 1408], F32); rMf = Reg()
      psS = [P.ps([128, 512]) for _ in range(4)]; rS = [Reg() for _ in range(4)]
      acc = P.ps([128, 512]); racc = Reg()
      zz = P.ps([128, 512]); rzz = Reg()
      tmp = [P.sb([128, 512], F32) for _ in range(2)]; rtmp = [Reg(), Reg()]
      Pt = [P.sb([128, 512], BF16) for _ in range(3)]; rPt = [Reg() for _ in range(3)]
      Rz = P.sb([128, 512], F32); rRz = Reg()
      k = 0
      for c in range(4):
          for s in range(2):
              P.dma("sp", Mf[:, s, :], MB_d.ap()[2 * c + s], reads=[], writes=[rMf], nowaw=False)
          P.op("dve", lambda e: e.tensor_tensor(out=Mf[:], in0=Mf[:], in1=maskB[:].unsqueeze(1).to_broadcast([128, 2, 1408]), op=ALU.add),
               reads=[rMf, rmb], writes=[rMf])
          for s in range(2):
              h8 = 2 * c + s
              P.op("pool", lambda e, s=s, h8=h8: e.tensor_copy(out=vpad[:, s, :, s * 64:(s + 1) * 64], in_=vb[:, :, h8 * 64:(h8 + 1) * 64]),
                   reads=[rvb], writes=[rvp])
          for qt in range(4):
              nb = 8
              for i in range(nb):
                  lb = 4 * qt + i
                  j0 = 896 - 128 * i
                  for s in range(2):
                      pS = psS[k % 4]; rs_ = rS[k % 4]; tm = tmp[k % 2]; rt = rtmp[k % 2]; pt = Pt[k % 3]; rpt = rPt[k % 3]; k += 1
                      lo, hi = s * 64, (s + 1) * 64
                      P.op("pe", lambda e, pS=pS, lo=lo, hi=hi, c=c, lb=lb, qt=qt: e.matmul(pS[:], kb[lo:hi, c, lb * 128:(lb + 1) * 128], qbT[lo:hi, c, qt * 512:(qt + 1) * 512], start=True, stop=True),
                           reads=[rkb, rqb], writes=[rs_])
                      P.op("dve", lambda e, pS=pS, tm=tm, s=s, j0=j0: e.scalar_tensor_tensor(out=tm[:], in0=pS[:], scalar=0.125, in1=Mf[:, s, j0:j0 + 512], op0=ALU.mult, op1=ALU.add),
                           reads=[rs_, rMf], writes=[rt])
                      P.op("act", lambda e, tm=tm, pt=pt, lb=lb: e.activation(out=pt[:], in_=tm[:], func=AF.Exp, bias=colL[:, lb:lb + 1], scale=1.0),
                           reads=[rt, rcl], writes=[rpt])
                      first = (i == 0 and s == 0); last = (i == nb - 1 and s == 1)
                      P.op("pe", lambda e, pt=pt, s=s, lb=lb, first=first, last=last: e.matmul(acc[:], vpad[:, s, lb, :], pt[:], start=first, stop=last),
                           reads=[rvp, rpt], writes=[racc])
                      P.op("pe", lambda e, pt=pt, s=s, first=first, last=last: e.matmul(zz[:], onesp[:, s, :], pt[:], start=first, stop=last),
                           reads=[rop, rpt], writes=[rzz])
              P.op("dve", lambda e: e.reciprocal(Rz[:], zz[:]), reads=[rzz], writes=[rRz])
              P.op("dve", lambda e, c=c, qt=qt: e.tensor_tensor(out=catT[:, 4 + c, qt * 512:(qt + 1) * 512], in0=acc[:], in1=Rz[:], op=ALU.mult),
                   reads=[racc, rRz], writes=[rcat])
      P.end_phase()

    if 'c' not in skip:
      P.begin_phase()
      C = consts(P)
      kl = P.sb([128, 4, 2560], BF16); rkl = Reg()
      vl = P.sb([128, 4, 20, 128], BF16); rvl = Reg()
      for h in range(4):
          P.dma("sp", kl[:, h, :], kloc_d.ap()[h], writes=[rkl])
          P.dma("sp", vl[:, h, :, :], vloc_d.ap()[h].rearrange("p (b n) -> p b n", n=128), writes=[rvl])
      MA = P.sb([128, 4, 1024], F32); rMA = Reg()
      for h in range(4):
          P.dma("sp", MA[:, h, :], MA_d.ap()[h], writes=[rMA])
      maskA = P.sb([128, 1024], F32)
      P.dma("sp", maskA[:], maskA_d.ap(), writes=[rMA])
      P.op("dve", lambda e: e.tensor_tensor(out=MA[:], in0=MA[:], in1=maskA[:].unsqueeze(1).to_broadcast([128, 4, 1024]), op=ALU.add),
           reads=[rMA], writes=[rMA])
      colA = P.sb([128, 128], F32); colL = P.sb([128, 20], F32); c15 = P.sb([128, 4], F32); rcol = Reg()
      P.dma("sp", colA[:], colA_d.ap(), writes=[rcol])
      P.dma("sp", colL[:], colL_d.ap(), writes=[rcol])
      P.dma("sp", c15[:], c15_d.ap(), writes=[rcol])
      colAh = P.sb([128, 4, 128], F32); colLh = P.sb([128, 4, 20], F32)
      for h in range(4):
          P.op("dve", lambda e, h=h: e.tensor_scalar(out=colAh[:, h, :], in0=colA[:], scalar1=c15[:, h:h + 1], scalar2=None, op0=ALU.add), reads=[rcol], writes=[rcol])
          P.op("dve", lambda e, h=h: e.tensor_scalar(out=colLh[:, h, :], in0=colL[:], scalar1=c15[:, h:h + 1], scalar2=None, op0=ALU.add), reads=[rcol], writes=[rcol])
      lamT = P.sb([64, 4], F32); rlam = Reg()
      P.dma("sp", lamT[:], lamT_d.ap(), writes=[rlam])
      pr = P.sb([64, 2], F32)
      P.op("dve", lambda e: e.tensor_tensor(out=pr[:, 0:1], in0=lamT[:, 0:1], in1=lamT[:, 1:2], op=ALU.mult), reads=[rlam], writes=[rlam])
      P.op("dve", lambda e: e.tensor_tensor(out=pr[:, 1:2], in0=lamT[:, 2:3], in1=lamT[:, 3:4], op=ALU.mult), reads=[rlam], writes=[rlam])
      psS = [P.ps([128, 512]) for _ in range(4)]; rS = [Reg() for _ in range(4)]
      acc = [P.ps([128, 512]) for _ in range(2)]; racc = [Reg(), Reg()]
      zz = [P.ps([128, 512]) for _ in range(2)]; rzz = [Reg(), Reg()]
      P.op("pe", lambda e: e.matmul(psS[0][:, 0:2], C["ones32"][0:64, :], pr[:], start=True, stop=True), reads=[rlam, C["r_ones32"]], writes=[rS[0]])
      el = P.sb([128, 2], F32); neglam = P.sb([128, 1], F32); rnl = Reg()
      P.op("act", lambda e: e.activation(out=el[:], in_=psS[0][:, 0:2], func=AF.Exp), reads=[rS[0]], writes=[rnl])
      P.op("dve", lambda e: e.tensor_tensor(out=neglam[:], in0=el[:, 1:2], in1=el[:, 0:1], op=ALU.subtract), reads=[rnl], writes=[rnl])
      P.op("dve", lambda e: e.tensor_scalar(out=neglam[:], in0=neglam[:], scalar1=-0.2, scalar2=None, op0=ALU.add), reads=[rnl], writes=[rnl])
      sg = P.sb([128, 1], F32); rsg = Reg()
      P.dma("sp", sg[:], sg_d.ap(), writes=[rsg])
      P.op("dve", lambda e: e.tensor_scalar(out=sg[:], in0=sg[:], scalar1=0.8, scalar2=None, op0=ALU.mult), reads=[rsg], writes=[rsg])

      kh = P.sb([128, L], BF16); rkh = Reg()
      vh = P.sb([128, 128, 128], BF16); rvh = Reg()
      tmp = [P.sb([128, 512], F32) for _ in range(2)]; rtmp = [Reg(), Reg()]
      Pt = [P.sb([128, 512], BF16) for _ in range(4)]; rPt = [Reg() for _ in range(4)]
      R0 = P.sb([128, 512], F32); R1 = P.sb([128, 512], F32); o = P.sb([128, 512], F32); rfin = Reg()
      k = 0
      for h in range(4):
          P.dma("sp", kh[:], kall_d.ap()[h], reads=[], writes=[rkh], nowaw=False)
          P.dma("sp", vh[:], vall_d.ap()[h].rearrange("p (b n) -> p b n", n=128), reads=[], writes=[rvh], nowaw=False)
          for qt in range(4):
              blocks = [("r", b) for b in range(128)] + [("l", lb) for lb in range(4 * qt + 8)]
              nblk = len(blocks)
              for bi, (kind, b) in enumerate(blocks):
                  near = (kind == "l" and b >= 4 * qt + 3)
                  for s in range(2):
                      pS = psS[k % 4]; rs_ = rS[k % 4]; tm = tmp[k % 2]; rt = rtmp[k % 2]; pt = Pt[k % 4]; rpt = rPt[k % 4]; k += 1
                      lo, hi = s * 64, (s + 1) * 64
                      q_ap = qaT[lo:hi, h, qt * 512:(qt + 1) * 512]
                      if kind == "r":
                          k_ap = kh[lo:hi, b * 128:(b + 1) * 128]; v_ap = vh[:, b, :]; rk, rv = rkh, rvh
                      else:
                          k_ap = kl[lo:hi, h, b * 128:(b + 1) * 128]; v_ap = vl[:, h, b, :]; rk, rv = rkl, rvl
                      P.op("pe", lambda e, pS=pS, k_ap=k_ap, q_ap=q_ap: e.matmul(pS[:], k_ap, q_ap, start=True, stop=True),
                           reads=[rk, rqa], writes=[rs_])
                      if near:
                          delta = 128 * (b - 4) - 512 * qt
                          j0 = 384 - delta
                          P.op("dve", lambda e, pS=pS, tm=tm, h=h, j0=j0: e.scalar_tensor_tensor(out=tm[:], in0=pS[:], scalar=0.125, in1=MA[:, h, j0:j0 + 512], op0=ALU.mult, op1=ALU.add),
                               reads=[rs_, rMA], writes=[rt])
                          P.op("act", lambda e, tm=tm, pt=pt, b=b: e.activation(out=pt[:], in_=tm[:], func=AF.Exp, bias=colL[:, b:b + 1], scale=1.0),
                               reads=[rt, rcol], writes=[rpt])
                      else:
                          cb = colAh[:, h, b:b + 1] if kind == "r" else colLh[:, h, b:b + 1]
                          P.op("act", lambda e, pS=pS, pt=pt, cb=cb: e.activation(out=pt[:], in_=pS[:], func=AF.Exp, bias=cb, scale=0.125),
                               reads=[rs_, rcol], writes=[rpt])
                      first = (bi == 0); last = (bi == nblk - 1)
                      P.op("pe", lambda e, pt=pt, s=s, v_ap=v_ap, first=first, last=last: e.matmul(acc[s][:], v_ap, pt[:], start=first, stop=last),
                           reads=[rv, rpt], writes=[racc[s]])
                      P.op("pe", lambda e, pt=pt, s=s, first=first, last=last: e.matmul(zz[s][:], C["ones16"][:], pt[:], start=first, stop=last),
                           reads=[C["r_ones16"], rpt], writes=[rzz[s]])
              P.op("dve", lambda e: e.reciprocal(R0[:], zz[0][:]), reads=[rzz[0]], writes=[rfin])
              P.op("dve", lambda e: e.reciprocal(R1[:], zz[1][:]), reads=[rzz[1]], writes=[rfin])
              P.op("dve", lambda e: e.tensor_tensor(out=R0[:], in0=acc[0][:], in1=R0[:], op=ALU.mult), reads=[racc[0], rfin], writes=[rfin])
              P.op("dve", lambda e: e.tensor_tensor(out=R1[:], in0=acc[1][:], in1=R1[:], op=ALU.mult), reads=[racc[1], rfin], writes=[rfin])
              P.op("dve", lambda e: e.scalar_tensor_tensor(out=o[:], in0=R1[:], scalar=neglam[:, 0:1], in1=R0[:], op0=ALU.mult, op1=ALU.add), reads=[rfin, rnl], writes=[rfin])
              P.op("act", lambda e: e.activation(out=R0[:], in_=o[:], func=AF.Square), reads=[rfin], writes=[rfin])
              pS = psS[k % 4]; rs_ = rS[k % 4]; k += 1
              P.op("pe", lambda e, pS=pS: e.matmul(pS[:], C["ones32"][:], R0[:], start=True, stop=True), reads=[rfin, C["r_ones32"]], writes=[rs_])
              P.op("act", lambda e, pS=pS: e.activation(out=R1[:], in_=pS[:], func=AF.Sqrt, scale=1.0 / 128, bias=C["eps"][:]), reads=[rs_, C["r_eps"]], writes=[rfin])
              P.op("dve", lambda e: e.reciprocal(R1[:], R1[:]), reads=[rfin], writes=[rfin])
              P.op("dve", lambda e, h=h, qt=qt: e.scalar_tensor_tensor(out=catT[:, h, qt * 512:(qt + 1) * 512], in0=o[:], scalar=sg[:, 0:1], in1=R1[:], op0=ALU.mult, op1=ALU.mult),
                   reads=[rfin, rsg], writes=[rcat])
      P.end_phase()

    P.begin_phase()
    xT = P.sb([128, 8, TL], F32); rx = Reg()
    for kc in range(8):
        P.dma("sp", xT[:, kc, :], xT_d.ap()[kc * 128:(kc + 1) * 128, :], writes=[rx])
    mt = P.sb([128, 48], F32); rmt = Reg()
    P.dma("sp", mt[:], modT_d.ap(), writes=[rmt])
    wo = P.sb([128, 8, 1024], BF16); rwo = Reg()
    load_cast(P, wo, rwo, wo_d.ap(), 1024, 1024)
    ps = [P.ps([128, 512]) for _ in range(4)]; rp = [Reg() for _ in range(4)]
    rxo = Reg()
    k = 0
    for mc in range(8):
        for tt in range(4):
            p = ps[k % 4]; r = rp[k % 4]; k += 1
            for kc in range(8):
                P.op("pe", lambda e, p=p, mc=mc, tt=tt, kc=kc: e.matmul(p[:], wo[:, kc, mc * 128:(mc + 1) * 128], catT[:, kc, tt * 512:(tt + 1) * 512], start=(kc == 0), stop=(kc == 7)),
                     reads=[rwo, rcat], writes=[r])
            P.op("dve", lambda e, p=p, mc=mc, tt=tt: e.scalar_tensor_tensor(out=xT[:, mc, tt * 512:(tt + 1) * 512], in0=p[:], scalar=mt[:, 16 + mc:17 + mc], in1=xT[:, mc, tt * 512:(tt + 1) * 512], op0=ALU.mult, op1=ALU.add),
                 reads=[r, rmt, rx], writes=[rx])
        P.dma("sp", xo_d.ap()[mc * 128:(mc + 1) * 128, :], xT[:, mc, :], reads=[rx], writes=[rxo])
    P.op("sp", lambda e: e.nop(), reads=[rxo])
    P.end_phase()
    return P.finish()


def t5_bucket_np(rel):
    nb = 16; max_exact = 8
    bucket = np.where(rel > 0, nb, 0)
    n = np.abs(rel)
    nf = np.maximum(n, 1).astype(np.float32)
    large = max_exact + (np.log(nf / max_exact) / math.log(128 / max_exact) * (nb - max_exact)).astype(np.int32)
    large = np.minimum(large, nb - 1)
    return bucket + np.where(n < max_exact, n, large)


def s2_consts():
    k = np.arange(128)[:, None]
    j = np.arange(1024)[None, :]
    relA = k - j + 384
    bidxA = t5_bucket_np(relA)
    maskA = np.where((k // 64) <= (j // 64) - 6, 0.0, NEG).astype(np.float32)
    j = np.arange(1408)[None, :]
    relB = k - j + 384
    bidxB = np.clip(relB, -128, 128) + 128
    dd = (k // 64) - (j // 64) + 6
    maskB = np.where((dd >= -8) & (dd <= 0), 0.0, NEG).astype(np.float32)
    return bidxA, maskA, bidxB, maskB


DFF = 2816
NF = 22


def build_ffn(final=False):
    P = Prog()
    din = lambda n, s, dt=F32: P.dram(n, s, dt, kind="ExternalInput")
    xT_d = din("xT", [1024, TL]); xh_d = din("xh", [128, 16]); modT_d = din("modT", [128, 48]); g_d = din("g", [128, 8])
    wi_d = din("wi", [1024, 2 * DFF]); wo_d = din("wo", [DFF, 1024])
    cw_d = din("cw", [128, NF * 3]); cb_d = din("cb", [128, NF]); hf_d = din("hf", [128, 1])
    if final:
        fg_d = din("fg", [128, 8])
    xo_d = P.dram("xo", [1024, TL], F32, kind="ExternalOutput")
    P.begin_phase()
    C = consts(P)
    gm, sh, gate, rmod = mod_prep(P, modT_d, g_d, 2)
    wi = P.sb([128, 8, 2 * DFF], BF16); rwi = Reg()
    wo = P.sb([128, NF, 1024], BF16); rwo = Reg()
    for kc in range(8):
        P.dma("pool", wi[:, kc, :], wi_d.ap()[kc * 128:(kc + 1) * 128, :], writes=[rwi], max_dma_last_dim=8192)
    for kc in range(NF):
        P.dma("pool", wo[:, kc, :], wo_d.ap()[kc * 128:(kc + 1) * 128, :], writes=[rwo], max_dma_last_dim=8192)
    cw = P.sb([128, NF, 3], F32); cb = P.sb([128, NF], F32); hf = P.sb([128, 1], F32); rcw = Reg()
    P.dma("sp", cw[:], cw_d.ap().rearrange("p (j t) -> p j t", t=3), writes=[rcw])
    P.dma("sp", cb[:], cb_d.ap(), writes=[rcw])
    P.dma("sp", hf[:], hf_d.ap(), writes=[rcw])
    if final:
        fg = P.sb([128, 8], F32); fz = P.sb([128, 8], F32); rfg = Reg()
        P.dma("sp", fg[:], fg_d.ap(), writes=[rfg])
        P.op("pool", lambda e: e.memset(fz[:], 0.0), writes=[rfg])
    ws = modnorm_ws(P)
    xt = [P.sb([128, 8, 512], F32)] * 2; rxt = [Reg()] * 2
    xh = P.sb([128, 8, 2], F32); rxh = Reg()
    h2 = P.sb([128, 8, 512], BF16); rh2 = Reg()
    act = P.sb([128, NF, 512], BF16); ract = Reg()
    carry = P.sb([128, NF, 2], F32); rcar = Reg()
    gbuf = [P.sb([128, 514], F32) for _ in range(2)]; rgb = [Reg(), Reg()]
    tt_ = [P.sb([128, 512], F32) for _ in range(2)]; rtt = [Reg(), Reg()]
    ps = [P.ps([128, 512]) for _ in range(6)]; rp = [Reg() for _ in range(6)]
    rxo = Reg()
    P.dma("sp", xh[:], xh_d.ap().rearrange("p (kc t) -> p kc t", t=2), writes=[rxh])
    modnorm(P, C, xh, rxh, 2, gm, sh, rmod, h2, rh2, ws=ws)
    k = 0
    for j in range(NF):
        p = ps[k % 6]; r = rp[k % 6]; k += 1
        for kc in range(8):
            P.op("pe", lambda e, p=p, j=j, kc=kc: e.matmul(p[:, 0:2], wi[:, kc, DFF + j * 128:DFF + (j + 1) * 128], h2[:, kc, 0:2], start=(kc == 0), stop=(kc == 7)),
                 reads=[rwi, rh2], writes=[r])
        P.op("dve", lambda e, p=p, j=j: e.tensor_scalar(out=carry[:, j, :], in0=p[:, 0:2], scalar1=hf[:, 0:1], scalar2=None, op0=ALU.mult),
             reads=[r, rcw], writes=[rcar])
    for tt in range(4):
        x_ = xt[tt % 2]; rx_ = rxt[tt % 2]
        for kc in range(8):
            P.dma("sp", x_[:, kc, :], xT_d.ap()[kc * 128:(kc + 1) * 128, tt * 512:(tt + 1) * 512], writes=[rx_])
        modnorm(P, C, x_, rx_, 512, gm, sh, rmod, h2, rh2, ws=ws)
        for j in range(NF):
            pv = ps[k % 6]; rv = rp[k % 6]; k += 1
            pg = ps[k % 6]; rg = rp[k % 6]; k += 1
            gb = gbuf[j % 2]; rg_b = rgb[j % 2]; t_ = tt_[j % 2]; rt_ = rtt[j % 2]
            for kc in range(8):
                P.op("pe", lambda e, pv=pv, j=j, kc=kc: e.matmul(pv[:], wi[:, kc, j * 128:(j + 1) * 128], h2[:, kc, :], start=(kc == 0), stop=(kc == 7)),
                     reads=[rwi, rh2], writes=[rv])
            for kc in range(8):
                P.op("pe", lambda e, pg=pg, j=j, kc=kc: e.matmul(pg[:], wi[:, kc, DFF + j * 128:DFF + (j + 1) * 128], h2[:, kc, :], start=(kc == 0), stop=(kc == 7)),
                     reads=[rwi, rh2], writes=[rg])
            P.op("act", lambda e, gb=gb, pg=pg: e.activation(out=gb[:, 2:514], in_=pg[:], func=AF.Copy), reads=[rg], writes=[rg_b])
            P.op("act", lambda e, gb=gb, j=j: e.activation(out=gb[:, 0:2], in_=carry[:, j, :], func=AF.Copy), reads=[rcar], writes=[rg_b])
            P.op("dve", lambda e, t_=t_, gb=gb, j=j: e.tensor_scalar(out=t_[:], in0=gb[:, 2:514], scalar1=cw[:, j, 2:3], scalar2=cb[:, j:j + 1], op0=ALU.mult, op1=ALU.add),
                 reads=[rg_b, rcw], writes=[rt_])
            P.op("dve", lambda e, t_=t_, gb=gb, j=j: e.scalar_tensor_tensor(out=t_[:], in0=gb[:, 1:513], scalar=cw[:, j, 1:2], in1=t_[:], op0=ALU.mult, op1=ALU.add),
                 reads=[rg_b, rcw, rt_], writes=[rt_])
            P.op("dve", lambda e, t_=t_, gb=gb, j=j: e.scalar_tensor_tensor(out=t_[:], in0=gb[:, 0:512], scalar=cw[:, j, 0:1], in1=t_[:], op0=ALU.mult, op1=ALU.add),
                 reads=[rg_b, rcw, rt_], writes=[rt_])
            P.op("act", lambda e, gb=gb, j=j: e.activation(out=carry[:, j, :], in_=gb[:, 512:514], func=AF.Copy), reads=[rg_b], writes=[rcar])
            P.op("act", lambda e, t_=t_: e.activation(out=t_[:], in_=t_[:], func=AF.Gelu), reads=[rt_], writes=[rt_])
            P.op("dve", lambda e, t_=t_, pv=pv, j=j: e.tensor_tensor(out=act[:, j, :], in0=pv[:], in1=t_[:], op=ALU.mult),
                 reads=[rv, rt_], writes=[ract])
        for mc in range(8):
            p = ps[k % 6]; r = rp[k % 6]; k += 1
            for kc in range(NF):
                P.op("pe", lambda e, p=p, mc=mc, kc=kc: e.matmul(p[:], wo[:, kc, mc * 128:(mc + 1) * 128], act[:, kc, :], start=(kc == 0), stop=(kc == NF - 1)),
                     reads=[rwo, ract], writes=[r])
            P.op("dve", lambda e, p=p, mc=mc, x_=x_: e.scalar_tensor_tensor(out=x_[:, mc, :], in0=p[:], scalar=gate[:, mc:mc + 1], in1=x_[:, mc, :], op0=ALU.mult, op1=ALU.add),
                 reads=[r, rmod, rx_], writes=[rx_])
        if final:
            modnorm(P, C, x_, rx_, 512, fg, fz, rfg, x_, rx_, ws=ws)
        for kc in range(8):
            P.dma("sp", xo_d.ap()[kc * 128:(kc + 1) * 128, tt * 512:(tt + 1) * 512], x_[:, kc, :], reads=[rx_], writes=[rxo])
    P.op("sp", lambda e: e.nop(), reads=[rxo])
    P.end_phase()
    return P.finish()


def ffn_inputs(d, i, modT, xfull, final=False):
    cwl = np.ascontiguousarray(np.transpose(d['ffn_conv_w'][i].reshape(3, NF, 128), (2, 1, 0)).reshape(128, NF * 3))
    cbl = np.ascontiguousarray(d['ffn_conv_b'][i].reshape(NF, 128).T)
    g = np.ascontiguousarray(d['norm2_g'][i].reshape(8, 128).T)
    ins = []
    for r in range(NCORE):
        t0 = r * TL
        xh = np.zeros((128, 16), np.float32)
        if r > 0:
            xh[:] = np.transpose(xfull[t0 - 2:t0].T.reshape(8, 128, 2), (1, 0, 2)).reshape(128, 16)
        dd = dict(xT=np.ascontiguousarray(xfull[t0:t0 + TL].T), xh=xh, modT=modT, g=g, wi=d['ffn_w_in'][i], wo=d['ffn_w_out'][i],
                  cw=cwl, cb=cbl, hf=np.full((128, 1), 0.0 if r == 0 else 1.0, np.float32))
        if final:
            dd["fg"] = np.ascontiguousarray(d['final_g'].reshape(8, 128).T)
        ins.append(dd)
    return ins


HALO_R = 4096
NBR = (HALO_R + TL) // 128
WD = 6528


def outproj_phase(P, catT, rcat, xT_d, modT_d, wo_d, xo_d):
    P.begin_phase()
    xT = P.sb([128, 8, TL], F32); rx = Reg()
    for kc in range(8):
        P.dma("sp", xT[:, kc, :], xT_d.ap()[kc * 128:(kc + 1) * 128, :], writes=[rx])
    mt = P.sb([128, 48], F32); rmt = Reg()
    P.dma("sp", mt[:], modT_d.ap(), writes=[rmt])
    wo = P.sb([128, 8, 1024], BF16); rwo = Reg()
    load_cast(P, wo, rwo, wo_d.ap(), 1024, 1024)
    ps = [P.ps([128, 512]) for _ in range(4)]; rp = [Reg() for _ in range(4)]
    rxo = Reg()
    k = 0
    for mc in range(8):
        for tt in range(4):
            p = ps[k % 4]; r = rp[k % 4]; k += 1
            for kc in range(8):
                P.op("pe", lambda e, p=p, mc=mc, tt=tt, kc=kc: e.matmul(p[:], wo[:, kc, mc * 128:(mc + 1) * 128], catT[:, kc, tt * 512:(tt + 1) * 512], start=(kc == 0), stop=(kc == 7)),
                     reads=[rwo, rcat], writes=[r])
            P.op("dve", lambda e, p=p, mc=mc, tt=tt: e.scalar_tensor_tensor(out=xT[:, mc, tt * 512:(tt + 1) * 512], in0=p[:], scalar=mt[:, 16 + mc:17 + mc], in1=xT[:, mc, tt * 512:(tt + 1) * 512], op0=ALU.mult, op1=ALU.add),
                 reads=[r, rmt, rx], writes=[rx])
        P.dma("sp", xo_d.ap()[mc * 128:(mc + 1) * 128, :], xT[:, mc, :], reads=[rx], writes=[rxo])
    P.op("sp", lambda e: e.nop(), reads=[rxo])
    P.end_phase()


def build_l4():
    P = Prog()
    din = lambda n, s, dt=F32: P.dram(n, s, dt, kind="ExternalInput")
    xT_d = din("xT", [1024, TL]); modT_d = din("modT", [128, 48]); g_d = din("g", [128, 8])
    wqk_d = din("wqk", [1024, 1024]); wv_d = din("wv", [1024, 512]); wgu_d = din("wgu", [1024, 1024])
    cos_d = din("cos", [128, TL]); sin_d = din("sin", [128, TL])
    qk_o = P.dram("qk", [4, 128, TL], BF16, kind="ExternalOutput")
    v_o = P.dram("v", [16, 128, 512], BF16, kind="ExternalOutput")
    gu_o = P.dram("gu", [8, 128, TL], BF16, kind="ExternalOutput")
    P.begin_phase()
    C = consts(P)
    xT = P.sb([128, 8, TL], F32); rx = Reg()
    for kc in range(8):
        P.dma("sp", xT[:, kc, :], xT_d.ap()[kc * 128:(kc + 1) * 128, :], writes=[rx])
    gm, sh, gate, rmod = mod_prep(P, modT_d, g_d, 1)
    hT = P.sb([128, 8, TL], BF16); rh = Reg()
    modnorm(P, C, xT, rx, TL, gm, sh, rmod, hT, rh)
    wqk = P.sb([128, 8, 1024], BF16); rwqk = Reg()
    wv = P.sb([128, 8, 512], BF16); rwv = Reg()
    wgu = P.sb([128, 8, 1024], BF16); rwgu = Reg()
    load_cast(P, wqk, rwqk, wqk_d.ap(), 1024, 1024)
    load_cast(P, wv, rwv, wv_d.ap(), 1024, 512)
    load_cast(P, wgu, rwgu, wgu_d.ap(), 1024, 1024)
    cos = P.sb([128, TL], F32); sin = P.sb([128, TL], F32); rtab = Reg()
    P.dma("sp", cos[:], cos_d.ap(), writes=[rtab]); P.dma("sp", sin[:], sin_d.ap(), writes=[rtab])
    ps = [P.ps([128, 512]) for _ in range(6)]; rp = [Reg() for _ in range(6)]
    st = [P.sb([128, TL], BF16) for _ in range(2)]; rst = [Reg(), Reg()]
    t1 = P.sb([128, 512], F32); t2 = P.sb([128, 512], F32); rt = Reg()
    rqk_o = Reg(); rv_o = Reg(); rgu_o = Reg()
    k = 0; si = 0
    for c in range(4):
        s_ = st[si % 2]; rs_ = rst[si % 2]; si += 1
        for tt in range(4):
            pa = ps[k % 6]; ra = rp[k % 6]; k += 1
            pb = ps[k % 6]; rb = rp[k % 6]; k += 1
            for kc in range(8):
                P.op("pe", lambda e, pa=pa, c=c, tt=tt, kc=kc: e.matmul(pa[:], wqk[:, kc, c * 128:(c + 1) * 128], hT[:, kc, tt * 512:(tt + 1) * 512], start=(kc == 0), stop=(kc == 7)),
                     reads=[rwqk, rh], writes=[ra])
            for kc in range(8):
                P.op("pe", lambda e, pb=pb, c=c, tt=tt, kc=kc: e.matmul(pb[:], wqk[:, kc, (c + 4) * 128:(c + 5) * 128], hT[:, kc, tt * 512:(tt + 1) * 512], start=(kc == 0), stop=(kc == 7)),
                     reads=[rwqk, rh], writes=[rb])
            P.op("dve", lambda e, pa=pa, tt=tt: e.tensor_tensor(out=t1[:], in0=pa[:], in1=cos[:, tt * 512:(tt + 1) * 512], op=ALU.mult), reads=[ra, rtab], writes=[rt])
            P.op("dve", lambda e, pb=pb, tt=tt: e.tensor_tensor(out=t2[:], in0=pb[:], in1=sin[:, tt * 512:(tt + 1) * 512], op=ALU.mult), reads=[rb, rtab], writes=[rt])
            P.op("dve", lambda e, s_=s_, tt=tt: e.tensor_tensor(out=s_[:, tt * 512:(tt + 1) * 512], in0=t1[:], in1=t2[:], op=ALU.add), reads=[rt], writes=[rs_])
        P.dma("sp", qk_o.ap()[c], s_[:], reads=[rs_], writes=[rqk_o])
    for c in range(8):
        s_ = st[si % 2]; rs_ = rst[si % 2]; si += 1
        for tt in range(4):
            pa = ps[k % 6]; ra = rp[k % 6]; k += 1
            for kc in range(8):
                P.op("pe", lambda e, pa=pa, c=c, tt=tt, kc=kc: e.matmul(pa[:], wgu[:, kc, c * 128:(c + 1) * 128], hT[:, kc, tt * 512:(tt + 1) * 512], start=(kc == 0), stop=(kc == 7)),
                     reads=[rwgu, rh], writes=[ra])
            fn = AF.Silu if c < 4 else AF.Copy
            P.op("act", lambda e, pa=pa, s_=s_, tt=tt, fn=fn: e.activation(out=s_[:, tt * 512:(tt + 1) * 512], in_=pa[:], func=fn), reads=[ra], writes=[rs_])
        P.dma("sp", gu_o.ap()[c], s_[:], reads=[rs_], writes=[rgu_o])
    vo = P.sb([128, 2, 512], BF16); rvos = [Reg(), Reg()]
    for tb in range(16):
        pa = ps[k % 6]; ra = rp[k % 6]; k += 1
        for kc in range(8):
            P.op("pe", lambda e, pa=pa, tb=tb, kc=kc: e.matmul(pa[:], hT[:, kc, tb * 128:(tb + 1) * 128], wv[:, kc, :], start=(kc == 0), stop=(kc == 7)),
                 reads=[rwv, rh], writes=[ra])
        P.op("act", lambda e, pa=pa, tb=tb: e.activation(out=vo[:, tb % 2, :], in_=pa[:], func=AF.Copy), reads=[ra], writes=[rvos[tb % 2]])
        P.dma("sp", v_o.ap()[tb], vo[:, tb % 2, :], reads=[rvos[tb % 2]], writes=[rv_o])
    P.op("sp", lambda e: e.nop(), reads=[rqk_o, rv_o, rgu_o])
    P.end_phase()
    return P.finish()


def build_l6():
    P = Prog()
    din = lambda n, s, dt=F32: P.dram(n, s, dt, kind="ExternalInput")
    xT_d = din("xT", [1024, TL]); modT_d = din("modT", [128, 48]); wo_d = din("wo", [1024, 1024])
    q_d = din("q", [2, 128, TL], BF16); kx_d = din("kx", [2, 128, HALO_R + TL], BF16); vx_d = din("vx", [128, NBR * 512], BF16)
    Dm_d = din("Dm", [4, 128, WD]); sg_d = din("sg", [4, 128, TL], BF16)
    y_d = din("y", [4, 128, TL], BF16); wg_d = din("wg", [512, 1024])
    xo_d = P.dram("xo", [1024, TL], F32, kind="ExternalOutput")
    catT = P.gsb([128, 8, TL], BF16); rcat = Reg()
    P.begin_phase()
    C = consts(P)
    qT = P.sb([128, 2, TL], BF16); rq = Reg()
    kx = P.sb([128, 2, HALO_R + TL], BF16); rkx = Reg()
    vx = P.sb([128, NBR, 512], BF16); rvx = Reg()
    sg = P.sb([128, 4, TL], BF16); rsg = Reg()
    for c in range(2):
        P.dma("sp", qT[:, c, :], q_d.ap()[c], writes=[rq])
        P.dma("sp", kx[:, c, :], kx_d.ap()[c], writes=[rkx])
    P.dma("sp", vx[:], vx_d.ap().rearrange("p (b n) -> p b n", n=512), writes=[rvx])
    for c in range(4):
        P.dma("sp", sg[:, c, :], sg_d.ap()[c], writes=[rsg])
    Dm = P.sb([128, WD], F32); rDm = Reg()
    psS = [P.ps([128, 512]) for _ in range(4)]; rS = [Reg() for _ in range(4)]
    acc = [P.ps([128, 512]) for _ in range(2)]; racc = [Reg(), Reg()]
    Pt = [P.sb([128, 512], BF16) for _ in range(4)]; rPt = [Reg() for _ in range(4)]
    sq = P.sb([128, 512], F32); rs_t = P.sb([128, 512], F32); o = P.sb([128, 512], F32); rfin = Reg()
    k = 0; it = 0
    for h in range(4):
        P.dma("sp", Dm[:], Dm_d.ap()[h], reads=[], writes=[rDm], nowaw=False)
        c = h // 2; lo = (h % 2) * 64; hi = lo + 64
        for qt in range(4):
            ac = acc[it % 2]; rac = racc[it % 2]; it += 1
            nblk = 4 * qt + 36
            for lb in range(nblk):
                pS = psS[k % 4]; rs_ = rS[k % 4]; pt = Pt[k % 4]; rpt = rPt[k % 4]; k += 1
                delta = 128 * (lb - 32) - 512 * qt
                j0 = 384 - delta
                P.op("pe", lambda e, pS=pS, c=c, lo=lo, hi=hi, lb=lb, qt=qt: e.matmul(pS[:], kx[lo:hi, c, lb * 128:(lb + 1) * 128], qT[lo:hi, c, qt * 512:(qt + 1) * 512], start=True, stop=True),
                     reads=[rkx, rq], writes=[rs_])
                P.op("dve", lambda e, pS=pS, pt=pt, j0=j0: e.tensor_tensor(out=pt[:], in0=pS[:], in1=Dm[:, j0:j0 + 512], op=ALU.mult),
                     reads=[rs_, rDm], writes=[rpt])
                P.op("pe", lambda e, ac=ac, pt=pt, lb=lb, h=h, first=(lb == 0), last=(lb == nblk - 1): e.matmul(ac[:], vx[:, lb, h * 128:(h + 1) * 128], pt[:], start=first, stop=last),
                     reads=[rvx, rpt], writes=[rac])
            P.op("act", lambda e, ac=ac: e.activation(out=sq[:], in_=ac[:], func=AF.Square), reads=[rac], writes=[rfin])
            pS = psS[k % 4]; rs_ = rS[k % 4]; k += 1
            P.op("pe", lambda e, pS=pS: e.matmul(pS[:], C["ones32"][:], sq[:], start=True, stop=True), reads=[rfin, C["r_ones32"]], writes=[rs_])
            P.op("act", lambda e, pS=pS: e.activation(out=rs_t[:], in_=pS[:], func=AF.Sqrt, scale=1.0 / 128, bias=C["eps"][:]), reads=[rs_, C["r_eps"]], writes=[rfin])
            P.op("dve", lambda e: e.reciprocal(rs_t[:], rs_t[:]), reads=[rfin], writes=[rfin])
            P.op("dve", lambda e, ac=ac: e.tensor_tensor(out=o[:], in0=ac[:], in1=rs_t[:], op=ALU.mult), reads=[rac, rfin], writes=[rfin])
            P.op("dve", lambda e, h=h, qt=qt: e.tensor_tensor(out=catT[:, h, qt * 512:(qt + 1) * 512], in0=o[:], in1=sg[:, h, qt * 512:(qt + 1) * 512], op=ALU.mult),
                 reads=[rfin, rsg], writes=[rcat])
    P.end_phase()
    P.begin_phase()
    yT = P.sb([128, 4, TL], BF16); ry = Reg()
    for c in range(4):
        P.dma("sp", yT[:, c, :], y_d.ap()[c], writes=[ry])
    wg = P.sb([128, 4, 1024], BF16); rwg = Reg()
    load_cast(P, wg, rwg, wg_d.ap(), 512, 1024)
    ps = [P.ps([128, 512]) for _ in range(6)]; rp = [Reg() for _ in range(6)]
    sgm = [P.sb([128, 512], F32) for _ in range(2)]; rsgm = [Reg(), Reg()]
    k = 0
    for m in range(4):
        for tt in range(4):
            pa = ps[k % 6]; ra = rp[k % 6]; k += 1
            pb = ps[k % 6]; rb = rp[k % 6]; k += 1
            s_ = sgm[k % 2]; rs2 = rsgm[k % 2]
            for kc in range(4):
                P.op("pe", lambda e, pa=pa, m=m, tt=tt, kc=kc: e.matmul(pa[:], wg[:, kc, m * 128:(m + 1) * 128], yT[:, kc, tt * 512:(tt + 1) * 512], start=(kc == 0), stop=(kc == 3)),
                     reads=[rwg, ry], writes=[ra])
            for kc in range(4):
                P.op("pe", lambda e, pb=pb, m=m, tt=tt, kc=kc: e.matmul(pb[:], wg[:, kc, 512 + m * 128:512 + (m + 1) * 128], yT[:, kc, tt * 512:(tt + 1) * 512], start=(kc == 0), stop=(kc == 3)),
                     reads=[rwg, ry], writes=[rb])
            P.op("act", lambda e, pb=pb, s_=s_: e.activation(out=s_[:], in_=pb[:], func=AF.Sigmoid), reads=[rb], writes=[rs2])
            P.op("dve", lambda e, pa=pa, s_=s_, m=m, tt=tt: e.tensor_tensor(out=catT[:, 4 + m, tt * 512:(tt + 1) * 512], in0=pa[:], in1=s_[:], op=ALU.mult),
                 reads=[ra, rs2], writes=[rcat])
    P.end_phase()
    outproj_phase(P, catT, rcat, xT_d, modT_d, wo_d, xo_d)
    return P.finish()


def rot_tables(r):
    inv = (1.0 / (np.float32(10000.0) ** (np.arange(0, 64, 2, dtype=np.float32) / np.float32(64)))).astype(np.float32)
    pos = np.arange(r * TL, (r + 1) * TL, dtype=np.float32)
    ang = (pos[:, None] * inv[None, :]).astype(np.float32)
    c = np.cos(ang).astype(np.float32); s = np.sin(ang).astype(np.float32)
    p = np.arange(128); f = p % 32; first = (p % 64) < 32
    cosT = np.ascontiguousarray(c[:, f].T)
    sinT = np.ascontiguousarray(np.where(first[None, :], -s[:, f], s[:, f]).T).astype(np.float32)
    return cosT, sinT


def decay_master():
    k = np.arange(128)[:, None].astype(np.float64); j = np.arange(WD)[None, :].astype(np.float64)
    rel = k - j + 384
    vis = (np.floor(k / 64) <= np.floor(j / 64) - 6)
    out = np.zeros((4, 128, WD), np.float32)
    for h in range(4):
        lg = np.log(np.float32(1.0) - np.float32(2.0) ** np.float32(-5.0 - h)).astype(np.float32)
        out[h] = np.where(vis, np.exp(np.float64(lg) * np.abs(rel)) * 0.125, 0.0).astype(np.float32)
    return out


def swap_cols(w):
    w4 = w.reshape(1024, 4, 2, 32)
    return np.ascontiguousarray(w4[:, :, ::-1, :].reshape(1024, 256))


TS = 512
NSEG = L // TS
PI = math.pi


def build_l5(nseg=NSEG):
    P = Prog()
    din = lambda n, s, dt=F32: P.dram(n, s, dt, kind="ExternalInput")
    uT_d = din("uT", [512, L], BF16); uo_d = din("uo", [64, L], BF16)
    lam_d = din("lam", [128, 4]); ls_d = din("ls", [128, 2])
    Wb_d = din("Wb", [512, 512]); Cm_d = din("Cm", [128, 256]); ds_d = din("dsel", [64, 1])
    y_o = P.dram("y", [64, L], BF16, kind="ExternalOutput")
    P.begin_phase()
    C = consts(P)
    lam = P.sb([128, 4], F32); ls = P.sb([128, 2], F32); dsel = P.sb([64, 1], F32); rpar = Reg()
    P.dma("sp", lam[:], lam_d.ap(), writes=[rpar]); P.dma("sp", ls[:], ls_d.ap(), writes=[rpar]); P.dma("sp", dsel[:], ds_d.ap(), writes=[rpar])
    Wb = P.sb([128, 4, 512], BF16); rWb = Reg()
    load_cast(P, Wb, rWb, Wb_d.ap(), 512, 512)
    Cm = P.sb([128, 256], BF16); rCm = Reg()
    P.dma("pool", Cm[:], Cm_d.ap(), writes=[rCm])
    sm = P.sb([128, 40], F32)
    col = lambda i: sm[:, 2 * i:2 * i + 2]
    D_, RHO, TH, SN, CS, AR, AI, NR, DEN, SR, SI, T0, T1, T2 = [col(i) for i in range(14)]
    lr, li = lam[:, 0:2], lam[:, 2:4]
    ki = P.sb([128, 2], I32)
    R = rpar

    def dv(fn):
        P.op("dve", fn, reads=[R], writes=[R])

    def ac(fn):
        P.op("act", fn, reads=[R], writes=[R])
    ac(lambda e: e.activation(out=D_, in_=ls[:], func=AF.Exp))
    dv(lambda e: e.tensor_tensor(out=T0, in0=lr, in1=D_, op=ALU.mult))
    ac(lambda e: e.activation(out=RHO, in_=T0, func=AF.Exp))
    dv(lambda e: e.tensor_tensor(out=TH, in0=li, in1=D_, op=ALU.mult))

    def sin_of(dst, shift):
        dv(lambda e: e.tensor_scalar(out=T0, in0=TH, scalar1=shift, scalar2=None, op0=ALU.add))
        dv(lambda e: e.tensor_scalar(out=T1, in0=T0, scalar1=1.0 / (2 * PI), scalar2=None, op0=ALU.mult))
        dv(lambda e: e.tensor_copy(out=ki[:], in_=T1))
        dv(lambda e: e.tensor_copy(out=T1, in_=ki[:]))
        dv(lambda e: e.scalar_tensor_tensor(out=T0, in0=T1, scalar=-2 * PI, in1=T0, op0=ALU.mult, op1=ALU.add))
        dv(lambda e: e.tensor_scalar(out=T1, in0=T0, scalar1=PI, scalar2=None, op0=ALU.is_gt))
        dv(lambda e: e.scalar_tensor_tensor(out=T0, in0=T1, scalar=-2 * PI, in1=T0, op0=ALU.mult, op1=ALU.add))
        dv(lambda e: e.tensor_scalar(out=T1, in0=T0, scalar1=-PI, scalar2=None, op0=ALU.is_lt))
        dv(lambda e: e.scalar_tensor_tensor(out=T0, in0=T1, scalar=2 * PI, in1=T0, op0=ALU.mult, op1=ALU.add))
        dv(lambda e: e.tensor_scalar(out=T0, in0=T0, scalar1=PI, scalar2=-PI, op0=ALU.min, op1=ALU.max))
        ac(lambda e: e.activation(out=dst, in_=T0, func=AF.Sin))
    sin_of(SN, 0.0)
    sin_of(CS, PI / 2)
    dv(lambda e: e.tensor_tensor(out=AR, in0=RHO, in1=CS, op=ALU.mult))
    dv(lambda e: e.tensor_tensor(out=AI, in0=RHO, in1=SN, op=ALU.mult))
    dv(lambda e: e.tensor_scalar(out=NR, in0=AR, scalar1=-1.0, scalar2=None, op0=ALU.add))
    dv(lambda e: e.tensor_tensor(out=DEN, in0=lr, in1=lr, op=ALU.mult))
    dv(lambda e: e.tensor_tensor(out=T0, in0=li, in1=li, op=ALU.mult))
    dv(lambda e: e.tensor_tensor(out=DEN, in0=DEN, in1=T0, op=ALU.add))
    dv(lambda e: e.reciprocal(DEN, DEN))
    dv(lambda e: e.tensor_tensor(out=T0, in0=NR, in1=lr, op=ALU.mult))
    dv(lambda e: e.tensor_tensor(out=T1, in0=AI, in1=li, op=ALU.mult))
    dv(lambda e: e.tensor_tensor(out=T0, in0=T0, in1=T1, op=ALU.add))
    dv(lambda e: e.tensor_tensor(out=SR, in0=T0, in1=DEN, op=ALU.mult))
    dv(lambda e: e.tensor_tensor(out=T0, in0=AI, in1=lr, op=ALU.mult))
    dv(lambda e: e.tensor_tensor(out=T1, in0=NR, in1=li, op=ALU.mult))
    dv(lambda e: e.tensor_tensor(out=T0, in0=T0, in1=T1, op=ALU.subtract))
    dv(lambda e: e.tensor_tensor(out=SI, in0=T0, in1=DEN, op=ALU.mult))
    Er = P.sb([128, 2, TS], F32); Ei = P.sb([128, 2, TS], F32); NEi = P.sb([128, 2, TS], F32)
    Zr = P.sb([128, 2, TS], F32); Zi = P.sb([128, 2, TS], F32); rho_t = P.sb([128, 2, TS], F32)
    tt = P.sb([128, TS], F32)
    for j in range(2):
        dv(lambda e, j=j: e.tensor_copy(out=Er[:, j, 0:1], in_=CS[:, j:j + 1]))
        dv(lambda e, j=j: e.tensor_copy(out=Ei[:, j, 0:1], in_=SN[:, j:j + 1]))
        Lc = 1
        while Lc < TS:
            mr = Er[:, j, Lc - 1:Lc]; mi = Ei[:, j, Lc - 1:Lc]
            dv(lambda e, mr=mr: e.tensor_copy(out=T0[:, 0:1], in_=mr))
            dv(lambda e, mi=mi: e.tensor_copy(out=T0[:, 1:2], in_=mi))
            dv(lambda e, j=j, Lc=Lc: e.tensor_scalar(out=tt[:, 0:Lc], in0=Ei[:, j, 0:Lc], scalar1=T0[:, 1:2], scalar2=None, op0=ALU.mult))
            dv(lambda e, j=j, Lc=Lc: e.scalar_tensor_tensor(out=Er[:, j, Lc:2 * Lc], in0=Er[:, j, 0:Lc], scalar=T0[:, 0:1], in1=tt[:, 0:Lc], op0=ALU.mult, op1=ALU.subtract))
            dv(lambda e, j=j, Lc=Lc: e.tensor_scalar(out=tt[:, 0:Lc], in0=Ei[:, j, 0:Lc], scalar1=T0[:, 0:1], scalar2=None, op0=ALU.mult))
            dv(lambda e, j=j, Lc=Lc: e.scalar_tensor_tensor(out=Ei[:, j, Lc:2 * Lc], in0=Er[:, j, 0:Lc], scalar=T0[:, 1:2], in1=tt[:, 0:Lc], op0=ALU.mult, op1=ALU.add))
            Lc *= 2
        dv(lambda e, j=j: e.tensor_scalar(out=tt[:], in0=Ei[:, j, :], scalar1=SI[:, j:j + 1], scalar2=None, op0=ALU.mult))
        dv(lambda e, j=j: e.scalar_tensor_tensor(out=Zr[:, j, :], in0=Er[:, j, :], scalar=SR[:, j:j + 1], in1=tt[:], op0=ALU.mult, op1=ALU.add))
        dv(lambda e, j=j: e.tensor_scalar(out=tt[:], in0=Ei[:, j, :], scalar1=SR[:, j:j + 1], scalar2=None, op0=ALU.mult))
        dv(lambda e, j=j: e.scalar_tensor_tensor(out=Zi[:, j, :], in0=Er[:, j, :], scalar=SI[:, j:j + 1], in1=tt[:], op0=ALU.mult, op1=ALU.subtract))
        dv(lambda e, j=j: e.tensor_scalar(out=NEi[:, j, :], in0=Ei[:, j, :], scalar1=-1.0, scalar2=None, op0=ALU.mult))
        dv(lambda e, j=j: e.memset(rho_t[:, j, :], 1.0))
        dv(lambda e, j=j: e.tensor_scalar(out=rho_t[:, j, :], in0=rho_t[:, j, :], scalar1=RHO[:, j:j + 1], scalar2=None, op0=ALU.mult))
    car = P.sb([128, 4], F32)
    dv(lambda e: e.memset(car[:], 0.0))
    us = [P.sb([128, 4, TS], BF16) for _ in range(2)]; rus = [Reg(), Reg()]
    uo = [P.sb([64, TS], BF16) for _ in range(2)]; ruo = [Reg(), Reg()]
    bu = [P.ps([128, 2, TS]) for _ in range(2)]; rbu = [Reg(), Reg()]
    yp = [P.ps([64, TS]) for _ in range(2)]; ryp = [Reg(), Reg()]
    a1 = P.sb([128, 2, TS], F32); a2 = P.sb([128, 2, TS], F32); zr = P.sb([128, 2, TS], F32); zi = P.sb([128, 2, TS], F32); rz = Reg()
    wr = P.sb([128, 2, TS], F32); wi = P.sb([128, 2, TS], F32); rw = Reg()
    xr = [P.sb([128, 2, TS], BF16) for _ in range(2)]; xi = [P.sb([128, 2, TS], BF16) for _ in range(2)]; rxx = [Reg(), Reg()]
    yv = P.sb([64, TS], F32); ryv = Reg()
    yo = [P.sb([64, TS], BF16) for _ in range(2)]; ryo = [Reg(), Reg()]
    ry_out = Reg()
    TT = lambda out, in0, in1, op: (lambda e: e.tensor_tensor(out=out, in0=in0, in1=in1, op=op))
    for sg in range(nseg):
        u_ = us[sg % 2]; ru_ = rus[sg % 2]; uo_ = uo[sg % 2]; ruo_ = ruo[sg % 2]
        for kc in range(4):
            P.dma("sp", u_[:, kc, :], uT_d.ap()[kc * 128:(kc + 1) * 128, sg * TS:(sg + 1) * TS], writes=[ru_])
        P.dma("sp", uo_[:], uo_d.ap()[:, sg * TS:(sg + 1) * TS], writes=[ruo_])
        for c in range(2):
            for j in range(2):
                for kc in range(4):
                    P.op("pe", lambda e, c=c, j=j, kc=kc, u_=u_: e.matmul(bu[c][:, j, :], Wb[:, kc, (j * 2 + c) * 128:(j * 2 + c + 1) * 128], u_[:, kc, :], start=(kc == 0), stop=(kc == 3)),
                         reads=[rWb, ru_], writes=[rbu[c]])
        br, bi = bu[0][:], bu[1][:]
        P.op("dve", TT(a1[:], br, Zr[:], ALU.mult), reads=[rbu[0], R], writes=[rz])
        P.op("dve", TT(a2[:], bi, Zi[:], ALU.mult), reads=[rbu[1], R], writes=[rz])
        P.op("dve", TT(zr[:], a1[:], a2[:], ALU.subtract), reads=[rz], writes=[rz])
        P.op("dve", TT(a1[:], bi, Zr[:], ALU.mult), reads=[rbu[1], R, rz], writes=[rz])
        P.op("dve", TT(a2[:], br, Zi[:], ALU.mult), reads=[rbu[0], R, rz], writes=[rz])
        P.op("dve", TT(zi[:], a1[:], a2[:], ALU.add), reads=[rz], writes=[rz])
        for j in range(2):
            P.op("dve", lambda e, j=j: e.tensor_tensor_scan(out=wr[:, j, :], data0=rho_t[:, j, :], data1=zr[:, j, :], initial=car[:, j:j + 1], op0=ALU.mult, op1=ALU.add),
                 reads=[rz, R], writes=[rw])
            P.op("dve", lambda e, j=j: e.tensor_tensor_scan(out=wi[:, j, :], data0=rho_t[:, j, :], data1=zi[:, j, :], initial=car[:, 2 + j:3 + j], op0=ALU.mult, op1=ALU.add),
                 reads=[rz, R], writes=[rw])
        xr_ = xr[sg % 2]; xi_ = xi[sg % 2]; rx_ = rxx[sg % 2]
        P.op("dve", TT(a1[:], Er[:], wr[:], ALU.mult), reads=[rw, R, rz], writes=[rz])
        P.op("dve", TT(a2[:], Ei[:], wi[:], ALU.mult), reads=[rw, R, rz], writes=[rz])
        P.op("dve", TT(xr_[:], a1[:], a2[:], ALU.subtract), reads=[rz], writes=[rx_])
        P.op("dve", TT(car[:, 0:2], a1[:, :, TS - 1], a2[:, :, TS - 1], ALU.subtract), reads=[rz], writes=[R])
        P.op("dve", TT(a1[:], NEi[:], wr[:], ALU.mult), reads=[rw, R, rz], writes=[rz])
        P.op("dve", TT(a2[:], Er[:], wi[:], ALU.mult), reads=[rw, R, rz], writes=[rz])
        P.op("dve", TT(xi_[:], a1[:], a2[:], ALU.subtract), reads=[rz], writes=[rx_])
        P.op("dve", TT(car[:, 2:4], a2[:, :, TS - 1], a1[:, :, TS - 1], ALU.subtract), reads=[rz], writes=[R])
        y_ = yp[sg % 2]; ry_ = ryp[sg % 2]
        n = 0
        for j in range(2):
            for c in range(2):
                src = xr_ if c == 0 else xi_
                P.op("pe", lambda e, y_=y_, j=j, c=c, src=src, n=n: e.matmul(y_[:], Cm[:, (j * 2 + c) * 64:(j * 2 + c + 1) * 64], src[:, j, :], start=(n == 0), stop=(n == 3)),
                     reads=[rCm, rx_], writes=[ry_])
                n += 1
        P.op("dve", lambda e, uo_=uo_, y_=y_: e.scalar_tensor_tensor(out=yv[:], in0=uo_[:], scalar=dsel[:, 0:1], in1=y_[:], op0=ALU.mult, op1=ALU.add),
             reads=[ruo_, ry_, R], writes=[ryv])
        yo_ = yo[sg % 2]; ryo_ = ryo[sg % 2]
        P.op("act", lambda e, yo_=yo_: e.activation(out=yo_[:], in_=yv[:], func=AF.Gelu), reads=[ryv], writes=[ryo_])
        P.dma("sp", y_o.ap()[:, sg * TS:(sg + 1) * TS], yo_[:], reads=[ryo_], writes=[ry_out])
    P.op("sp", lambda e: e.nop(), reads=[ry_out])
    P.end_phase()
    return P.finish()


def l5_inputs(d, uT_all):
    lre = d['s5_lam_re'][0]; lim = d['s5_lam_im'][0]; lst = d['s5_log_step'][0]
    bre = d['s5_b_re'][0]; bim = d['s5_b_im'][0]; cre = d['s5_c_re'][0]; cim = d['s5_c_im'][0]; dd = d['s5_d'][0]
    ins = []
    for r in range(NCORE):
        lam = np.zeros((128, 4), np.float32); ls = np.zeros((128, 2), np.float32)
        Wb = np.zeros((512, 2, 2, 128), np.float32); Cm = np.zeros((128, 2, 2, 64), np.float32)
        for j in range(2):
            for gl in range(2):
                g = 4 * r + 2 * j + gl
                rows = slice(gl * 64, (gl + 1) * 64)
                lam[rows, j] = lre[g]; lam[rows, 2 + j] = lim[g]; ls[rows, j] = lst[g]
                Wb[16 * g:16 * g + 16, j, 0, rows] = bre[g].T
                Wb[16 * g:16 * g + 16, j, 1, rows] = bim[g].T
                g4 = 2 * j + gl
                Cm[rows, j, 0, g4 * 16:(g4 + 1) * 16] = cre[g].T
                Cm[rows, j, 1, g4 * 16:(g4 + 1) * 16] = cim[g].T
        ins.append(dict(uT=uT_all, uo=np.ascontiguousarray(uT_all[64 * r:64 * r + 64]), lam=lam, ls=ls,
                        Wb=np.ascontiguousarray(Wb.reshape(512, 512)), Cm=np.ascontiguousarray(Cm.reshape(128, 256)),
                        dsel=np.ascontiguousarray(dd[64 * r:64 * r + 64].reshape(64, 1))))
    return ins


import numpy as np, ml_dtypes
bf16 = ml_dtypes.bfloat16

def lay8(v):
    return np.ascontiguousarray(np.asarray(v, np.float32).reshape(-1, 128).T)

def s2_inputs(d, modT0, kT, v):
    bidxA, maskA, bidxB, maskB = s2_consts()
    t5 = d['t5_table']; rb = d['band_rel_bias'][0]
    MA = np.ascontiguousarray(np.transpose(t5[bidxA], (2, 0, 1))).astype(np.float32)
    MB = np.ascontiguousarray(rb[:, bidxB]).astype(np.float32)
    c15 = np.ascontiguousarray(np.broadcast_to(t5[15][None, :], (128, 4))).astype(np.float32)
    kall = np.ascontiguousarray(np.transpose(kT[:, 0:4], (1, 2, 0, 3)).reshape(4, 128, L))
    va = v[:, :, :, 0:512].reshape(8, 16, 128, 4, 128)
    vall = np.ascontiguousarray(np.transpose(va, (3, 2, 0, 1, 4)).reshape(4, 128, 128 * 128))
    kball = np.transpose(kT[:, 4:8], (1, 2, 0, 3)).reshape(4, 128, L)
    vball = np.transpose(v[:, :, :, 512:1024], (2, 0, 1, 3)).reshape(128, 128, 512)
    x = d['x'][0]
    ins = []
    for r in range(NCORE):
        t0 = r * TL
        def ext_cols(a):
            out = np.zeros(a.shape[:-1] + (2560,), a.dtype)
            lo = t0 - 512
            if lo >= 0:
                out[..., :] = a[..., lo:t0 + TL]
            else:
                out[..., 512:] = a[..., 0:TL]
            return out
        def ext_blk(a):
            out = np.zeros((a.shape[0], 20, a.shape[2]), a.dtype)
            b0 = r * 16 - 4
            if b0 >= 0:
                out[:] = a[:, b0:b0 + 20]
            else:
                out[:, 4:] = a[:, 0:16]
            return out
        kloc = ext_cols(kall); kbloc = ext_cols(kball)
        vloc = np.stack([ext_blk(vall[h].reshape(128, 128, 128)) for h in range(4)]).reshape(4, 128, 20 * 128)
        vbloc = ext_blk(vball).reshape(128, 20 * 512)
        colA = np.where(np.arange(128)[None, :] < 16 * r - 4, 0.0, NEG).astype(np.float32) * np.ones((128, 1), np.float32)
        colL = np.zeros((128, 20), np.float32)
        if r == 0:
            colL[:, 0:4] = NEG
        ins.append(dict(xT=np.ascontiguousarray(x[t0:t0 + TL].T), modT=modT0, g=lay8(d['norm1_g'][0]), w=d['ev_w_in'][0], wo=d['ev_w_out'][0],
                        kall=kall, vall=vall, kloc=np.ascontiguousarray(kloc), vloc=np.ascontiguousarray(vloc),
                        kbloc=np.ascontiguousarray(kbloc), vbloc=np.ascontiguousarray(vbloc),
                        MA=MA, maskA=maskA, colA=np.ascontiguousarray(colA), c15=c15, colL=colL, MB=MB, maskB=maskB,
                        lamT=np.ascontiguousarray(d['diff_lambda'][0].T), sg=np.ascontiguousarray(d['diff_subln_g'][0].reshape(128, 1))))
    return ins


import ml_dtypes


def l6_inputs(d, modT1, x1, qk, v, gu, y):
    kall = np.transpose(qk[:, 2:4], (1, 2, 0, 3)).reshape(2, 128, L)
    vall = np.transpose(v, (2, 0, 1, 3)).reshape(128, 128, 512)
    yfull = y.reshape(512, L)
    Dm = decay_master()
    ins = []
    for r in range(NCORE):
        t0 = r * TL
        kx = np.zeros((2, 128, HALO_R + TL), kall.dtype)
        lo = t0 - HALO_R
        if lo >= 0:
            kx[:] = kall[:, :, lo:t0 + TL]
        else:
            kx[:, :, -lo:] = kall[:, :, 0:t0 + TL]
        vx = np.zeros((128, NBR, 512), vall.dtype)
        b0 = r * 16 - HALO_R // 128
        if b0 >= 0:
            vx[:] = vall[:, b0:b0 + NBR]
        else:
            vx[:, -b0:] = vall[:, 0:r * 16 + 16]
        ins.append(dict(xT=np.ascontiguousarray(x1[t0:t0 + TL].T), modT=modT1, wo=d['od_w_out'][0],
                        q=np.ascontiguousarray(qk[r, 0:2]), kx=kx, vx=np.ascontiguousarray(vx.reshape(128, NBR * 512)),
                        Dm=Dm, sg=np.ascontiguousarray(gu[r, 0:4]),
                        y=np.ascontiguousarray(yfull[:, t0:t0 + TL].reshape(4, 128, TL)), wg=d['s5_glu_w'][0]))
    return ins


_NC_CACHE = {}


def _nc(name, fn):
    return fn()


def _run(nc, ins):
    res = run_bass_kernel_spmd(nc, ins, core_ids=list(range(NCORE)))
    return res.results


def kernel(**inputs):
    d = {k: np.asarray(v) for k, v in inputs.items()}
    x = d['x'][0]
    c = d['c'][0]
    cT = np.ascontiguousarray(c.reshape(8, 128).T)
    ins = [{"cT": cT, "mw": np.ascontiguousarray(d['mod_w'][:, :, r * 768:(r + 1) * 768]),
            "mb": np.ascontiguousarray(d['mod_b'][:, r * 768:(r + 1) * 768])} for r in range(NCORE)]
    res = _run(build_s0(), ins)
    mod = np.concatenate([r["out"] for r in res], axis=1)
    modT = [np.ascontiguousarray(mod[i].reshape(48, 128).T) for i in range(2)]
    ins = [dict(xT=np.ascontiguousarray(x[r * TL:(r + 1) * TL].T), modT=modT[0], g=lay8(d['norm1_g'][0]), w=d['ev_w_in'][0]) for r in range(NCORE)]
    res = _run(build_s1(), ins)
    kT = np.stack([r["kT"] for r in res]); v = np.stack([r["v"] for r in res])
    res = _run(build_s2(), s2_inputs(d, modT[0], kT, v))
    xmid = np.concatenate([r["xo"].T for r in res], axis=0)
    res = _run(build_ffn(False), ffn_inputs(d, 0, modT[0], xmid))
    x1 = np.ascontiguousarray(np.concatenate([r["xo"].T for r in res], axis=0))
    w = d['od_w_in'][0]
    wqk = np.ascontiguousarray(np.concatenate([w[:, 0:256], w[:, 256:512], swap_cols(w[:, 0:256]), swap_cols(w[:, 256:512])], axis=1))
    wv = np.ascontiguousarray(w[:, 512:1024]); wgu = np.ascontiguousarray(w[:, 1024:2048])
    ins = []
    for r in range(NCORE):
        cs, sn = rot_tables(r)
        ins.append(dict(xT=np.ascontiguousarray(x1[r * TL:(r + 1) * TL].T), modT=modT[1], g=lay8(d['norm1_g'][1]), wqk=wqk, wv=wv, wgu=wgu, cos=cs, sin=sn))
    res = _run(build_l4(), ins)
    qk = np.stack([r["qk"] for r in res]); v1 = np.stack([r["v"] for r in res]); gu = np.stack([r["gu"] for r in res])
    uT_all = np.ascontiguousarray(np.transpose(gu[:, 4:8], (1, 2, 0, 3)).reshape(512, L))
    res = _run(build_l5(), l5_inputs(d, uT_all))
    y = np.stack([r["y"] for r in res])
    res = _run(build_l6(), l6_inputs(d, modT[1], x1, qk, v1, gu, y))
    xmid1 = np.concatenate([r["xo"].T for r in res], axis=0)
    res = _run(build_ffn(True), ffn_inputs(d, 1, modT[1], xmid1, final=True))
    out = np.concatenate([r["xo"].T for r in res], axis=0)
    return np.ascontiguousarray(out.reshape(1, L, D).astype(np.float32))
```
